# Optimizing a Trainium2 kernel written in Bass

```python
import jax, jax.numpy as jnp
from jax import lax
import numpy as np

D_MODEL = 2048
BATCH = 4
SEQ = 4096
DEPTH = 1

HEAD_DIM = 128
DILATION_PAIRS = ((128, 1), (512, 4), (2048, 16))
N_GROUPS = len(DILATION_PAIRS)
HEADS_PER_GROUP = D_MODEL // (4 * HEAD_DIM)
N_ATTN_HEADS = N_GROUPS * HEADS_PER_GROUP
ATTN_WIDTH = N_ATTN_HEADS * HEAD_DIM
ATTN_OUT_WIDTH = HEADS_PER_GROUP * HEAD_DIM
ROT_DIM = HEAD_DIM // 4
ROPE_THETA = 500000.0
POOL_WINDOWS = (2, 4, 8, 16)
POOL_WIDTH = D_MODEL // 4
POOL_GROUP_WIDTH = POOL_WIDTH // len(POOL_WINDOWS)
IN_WIDTH = 3 * ATTN_WIDTH + POOL_WIDTH
D_FF = 3 * D_MODEL
CONV_WIDTH = 3
NORM_EPS = 1e-6
NEG_INF = -1e30

kernel_name = 'hybrid_dilated_attn_pool_convffn_encoder'


def rmsnorm(x, g):
    xf = x.astype(jnp.float32)
    y = xf * lax.rsqrt(jnp.mean(xf * xf, axis=-1, keepdims=True) + NORM_EPS)
    return (y * g.astype(jnp.float32)).astype(x.dtype)


def partial_rotary(x, pos):
    half = ROT_DIM // 2
    inv_freq = ROPE_THETA ** (-jnp.arange(0, ROT_DIM, 2, dtype=jnp.float32) / ROT_DIM)
    ang = pos.astype(jnp.float32)[:, None] * inv_freq[None, :]
    cos = jnp.cos(ang)[None, :, None, :]
    sin = jnp.sin(ang)[None, :, None, :]
    xf = x.astype(jnp.float32)
    x1, x2, rest = xf[..., :half], xf[..., half:ROT_DIM], xf[..., ROT_DIM:]
    out = jnp.concatenate([x1 * cos - x2 * sin, x2 * cos + x1 * sin, rest], axis=-1)
    return out.astype(x.dtype)


def banded_attention(q, k, v, radius):
    L, Dh = q.shape[-2], q.shape[-1]
    lead = q.shape[:-2]
    blk = radius
    nb = -(-L // blk)
    Lp = nb * blk
    npad = [(0, 0)] * len(lead)
    qb = jnp.pad(q, npad + [(0, Lp - L), (0, 0)]).reshape(*lead, nb, blk, Dh)
    kb = jnp.pad(k, npad + [(blk, Lp - L + blk), (0, 0)]).reshape(*lead, nb + 2, blk, Dh)
    vb = jnp.pad(v, npad + [(blk, Lp - L + blk), (0, 0)]).reshape(*lead, nb + 2, blk, Dh)

    def window(xb):
        return jnp.concatenate([xb[..., :-2, :, :], xb[..., 1:-1, :, :], xb[..., 2:, :, :]], axis=-2)

    kw, vw = window(kb), window(vb)
    s = jnp.einsum('...nqd,...nkd->...nqk', qb.astype(jnp.float32), kw.astype(jnp.float32)) * (Dh ** -0.5)
    qi = jnp.arange(nb)[:, None, None] * blk + jnp.arange(blk)[None, :, None]
    kj = jnp.arange(nb)[:, None, None] * blk - blk + jnp.arange(3 * blk)[None, None, :]
    valid = (jnp.abs(kj - qi) <= radius) & (kj >= 0) & (kj < L)
    s = jnp.where(valid, s, NEG_INF)
    m = jnp.max(s, axis=-1, keepdims=True)
    p = jnp.where(valid, jnp.exp(s - m), 0.0)
    den = jnp.sum(p, axis=-1, keepdims=True)
    o = jnp.einsum('...nqk,...nkd->...nqd', p, vw.astype(jnp.float32)) / den
    lse = (m + jnp.log(den))[..., 0]
    o = o.reshape(*lead, Lp, Dh)[..., :L, :]
    lse = lse.reshape(*lead, Lp)[..., :L]
    return o, lse


def dilated_attention(q, k, v, dilation, radius):
    B, H, S, Dh = q.shape
    L = S // dilation

    def split(t):
        return t.reshape(B, H, L, dilation, Dh).transpose(0, 1, 3, 2, 4)

    o, lse = banded_attention(split(q), split(k), split(v), radius)
    o = o.transpose(0, 1, 3, 2, 4).reshape(B, H, S, Dh)
    lse = lse.transpose(0, 1, 3, 2).reshape(B, H, S)
    return o, lse


def multiscale_pool(p):
    S = p.shape[1]
    pf = p.astype(jnp.float32)
    c = jnp.concatenate([jnp.zeros_like(pf[:, :1]), jnp.cumsum(pf, axis=1)], axis=1)
    t = jnp.arange(S)
    outs = []
    for g, w in enumerate(POOL_WINDOWS):
        sl = slice(g * POOL_GROUP_WIDTH, (g + 1) * POOL_GROUP_WIDTH)
        cg = c[..., sl]
        lo = jnp.maximum(t - w // 2, 0)
        hi = jnp.minimum(t + w // 2, S)
        cnt = (hi - lo).astype(jnp.float32)[None, :, None]
        mean = (jnp.take(cg, hi, axis=1) - jnp.take(cg, lo, axis=1)) / cnt
        outs.append(mean - pf[..., sl])
    return jnp.concatenate(outs, axis=-1).astype(p.dtype)


def depthwise_conv3(a, w, b):
    ap = jnp.pad(a, ((0, 0), (1, 1), (0, 0)))
    return ap[:, :-2] * w[0] + ap[:, 1:-1] * w[1] + ap[:, 2:] * w[2] + b


def setup_inputs(seed: int = 0) -> dict:
    key = jax.random.key(seed)
    ks = jax.random.split(key, 16)
    f32 = jnp.float32

    def dense(k, shape, fan_in):
        return jax.random.normal(k, shape, f32) * (fan_in ** -0.5)

    return {
        'x': jax.random.normal(ks[0], (BATCH, SEQ, D_MODEL), f32),
        'norm_mix_g': 1.0 + 0.02 * jax.random.normal(ks[1], (D_MODEL,), f32),
        'w_in': dense(ks[2], (D_MODEL, IN_WIDTH), D_MODEL),
        'w_attn_out': dense(ks[3], (ATTN_OUT_WIDTH, D_MODEL), ATTN_OUT_WIDTH),
        'pool_w': dense(ks[4], (len(POOL_WINDOWS), POOL_GROUP_WIDTH, POOL_GROUP_WIDTH), POOL_GROUP_WIDTH),
        'pool_scale': 1.0 + 0.02 * jax.random.normal(ks[5], (POOL_WIDTH,), f32),
        'w_pool_out': dense(ks[6], (POOL_WIDTH, D_MODEL), POOL_WIDTH),
        'w_gate': dense(ks[7], (D_MODEL, 2 * D_MODEL), D_MODEL),
        'w_out': dense(ks[8], (D_MODEL, D_MODEL), D_MODEL),
        'norm_ffn_g': 1.0 + 0.02 * jax.random.normal(ks[9], (D_MODEL,), f32),
        'w_up': dense(ks[10], (D_MODEL, 2 * D_FF), D_MODEL),
        'conv_w': dense(ks[11], (CONV_WIDTH, D_FF), CONV_WIDTH),
        'conv_b': 0.01 * jax.random.normal(ks[12], (D_FF,), f32),
        'w_down': dense(ks[13], (D_FF, D_MODEL), D_FF),
        'norm_final_g': 1.0 + 0.02 * jax.random.normal(ks[14], (D_MODEL,), f32),
    }


def reference(x, norm_mix_g, w_in, w_attn_out, pool_w, pool_scale, w_pool_out, w_gate, w_out,
              norm_ffn_g, w_up, conv_w, conv_b, w_down, norm_final_g):
    B, S, _ = x.shape
    pos = jnp.arange(S)
    h = x
    for _layer in range(DEPTH):
        u = rmsnorm(h, norm_mix_g)
        proj = u @ w_in
        q, k, v, p = jnp.split(proj, [ATTN_WIDTH, 2 * ATTN_WIDTH, 3 * ATTN_WIDTH], axis=-1)
        q = partial_rotary(q.reshape(B, S, N_ATTN_HEADS, HEAD_DIM), pos)
        k = partial_rotary(k.reshape(B, S, N_ATTN_HEADS, HEAD_DIM), pos)
        v = v.reshape(B, S, N_ATTN_HEADS, HEAD_DIM)

        outs, lses = [], []
        for g, (window, dil) in enumerate(DILATION_PAIRS):
            sl = slice(g * HEADS_PER_GROUP, (g + 1) * HEADS_PER_GROUP)
            qg = q[:, :, sl].transpose(0, 2, 1, 3)
            kg = k[:, :, sl].transpose(0, 2, 1, 3)
            vg = v[:, :, sl].transpose(0, 2, 1, 3)
            o, l = dilated_attention(qg, kg, vg, dil, window // 2 // dil)
            outs.append(o)
            lses.append(l)
        alpha = jax.nn.softmax(jnp.stack(lses, axis=0), axis=0)
        attn = jnp.sum(alpha[..., None] * jnp.stack(outs, axis=0), axis=0)
        attn = attn.transpose(0, 2, 1, 3).reshape(B, S, ATTN_OUT_WIDTH).astype(h.dtype)
        y_a = attn @ w_attn_out

        pooled = multiscale_pool(p).reshape(B, S, len(POOL_WINDOWS), POOL_GROUP_WIDTH)
        pm = jnp.einsum('bsgc,gcd->bsgd', pooled, pool_w).reshape(B, S, POOL_WIDTH) * pool_scale
        y_b = pm @ w_pool_out

        g_a, g_b = jnp.split(jax.nn.sigmoid(u @ w_gate), 2, axis=-1)
        h = h + (g_a * y_a + g_b * y_b) @ w_out

        z = rmsnorm(h, norm_ffn_g)
        a, b = jnp.split(z @ w_up, 2, axis=-1)
        a = depthwise_conv3(a, conv_w, conv_b)
        h = h + (jax.nn.gelu(a, approximate=False) * b) @ w_down
    return rmsnorm(h, norm_final_g)
```

```python
import numpy as np
import ml_dtypes
from contextlib import ExitStack
import concourse.bass as bass
import concourse.mybir as mybir
from concourse.bass_utils import run_bass_kernel_spmd

F32 = mybir.dt.float32
BF = mybir.dt.bfloat16
AF = mybir.ActivationFunctionType
ALU = mybir.AluOpType

ENGS = ('pe', 'act', 'dve', 'pool', 'sp')
DMA_RING = {'sp': 24, 'pool': 24, 'act': 4}
SEM_LIM = 20000

D = 2048
KC = 16
S_LEN = 4096
OWN = 2048
HL = 1088
NCOL = 4224
TW = 256
NT = 17
QA0 = HL - 16
NQA = 2080
NQ = 2050
DILS = (1, 4, 16)
DFF = 6144
EPS = 1e-6


class Op:
    __slots__ = ('eng', 'fn', 'deps', 'idx', 'mark', 'dma', 'dk', 'mno')


class Sched:
    def __init__(self):
        self.ops = {e: [] for e in ENGS}
        self.w = {}
        self.r = {}
        self.ndma = {e: 0 for e in ENGS}

    def add(self, eng, fn, reads=(), writes=(), dma=False):
        op = Op()
        op.eng = eng
        op.fn = fn
        op.dma = dma
        op.mark = False
        op.mno = -1
        reads = list(reads) + ['PHASE']
        deps = []
        for k in reads:
            deps += self.w.get(k, [])
        for k in writes:
            deps += self.w.get(k, [])
            rr = self.r.get(k)
            if rr:
                deps += list(rr[0].values()) + rr[1]
        seen = set()
        op.deps = []
        for d in deps:
            if id(d) not in seen and d is not op:
                seen.add(id(d))
                op.deps.append(d)
        for k in reads:
            rr = self.r.setdefault(k, [{}, []])
            if dma:
                rr[1].append(op)
            else:
                rr[0][eng] = op
        for k in writes:
            self.w[k] = [op]
            self.r[k] = [{}, []]
        if dma:
            op.dk = self.ndma[eng]
            self.ndma[eng] += 1
        op.idx = len(self.ops[eng])
        self.ops[eng].append(op)
        return op

    def emit(self, nc, stack):
        for e in ENGS:
            for op in self.ops[e]:
                for d in op.deps:
                    if not d.dma:
                        if d.eng == 'pe' and op.eng == 'pe':
                            continue
                        d.mark = True
        nmark = {}
        for e in ENGS:
            n = 0
            for op in self.ops[e]:
                if op.mark and not op.dma:
                    op.mno = n
                    n += 1
            nmark[e] = n
        esem = {}
        for e in ENGS:
            ns = (nmark[e] + SEM_LIM - 1) // SEM_LIM
            esem[e] = [stack.enter_context(nc.semaphore("tl_%s_%d" % (e, i))) for i in range(ns)]
        dsem = {}
        for e in ENGS:
            if self.ndma[e]:
                R = min(DMA_RING[e], self.ndma[e])
                dsem[e] = [stack.enter_context(nc.semaphore("dm_%s_%d" % (e, i))) for i in range(R)]
        handles = {'pe': 'tensor', 'act': 'scalar', 'dve': 'vector', 'pool': 'gpsimd', 'sp': 'sync'}
        sched = self
        block = stack.enter_context(nc.Block())

        def make(e):
            def body(h):
                waited_c = {}
                waited_d = {}
                for op in sched.ops[e]:
                    need_c = {}
                    need_d = {}
                    for d in op.deps:
                        if d.dma:
                            R = len(dsem[d.eng])
                            key = (d.eng, d.dk % R)
                            val = 16 * (d.dk // R + 1)
                            if waited_d.get(key, 0) < val:
                                need_d[key] = max(need_d.get(key, 0), val)
                        else:
                            if d.eng == 'pe' and e == 'pe':
                                continue
                            if waited_c.get(d.eng, -1) < d.mno:
                                need_c[d.eng] = max(need_c.get(d.eng, -1), d.mno)
                    if op.dma:
                        R = len(dsem[e])
                        if op.dk >= R:
                            key = (e, op.dk % R)
                            val = 16 * (op.dk // R)
                            if waited_d.get(key, 0) < val:
                                need_d[key] = max(need_d.get(key, 0), val)
                    for pe_, m in need_c.items():
                        h.wait_ge(esem[pe_][m // SEM_LIM], m % SEM_LIM + 1)
                        waited_c[pe_] = m
                    for key, val in need_d.items():
                        h.wait_ge(dsem[key[0]][key[1]], val)
                        waited_d[key] = val
                    if op.fn is None:
                        continue
                    ins = op.fn(h)
                    if op.dma:
                        R = len(dsem[e])
                        ins.then_inc(dsem[e][op.dk % R], 16)
                    elif op.mark:
                        ins.then_inc(esem[e][op.mno // SEM_LIM], 1)
            return body

        for e in ENGS:
            if sched.ops[e]:
                getattr(block, handles[e])(make(e))


class Reg:
    def __init__(self, handle, F, base):
        self.t = handle
        self.F = F
        self.base = base

    def ap(self, off=0, dims=None, p0=0, n=128):
        return bass.AP(self.t, p0 * self.F + self.base + off, [[self.F, n]] + [list(d) for d in dims])


ARENA_BYTES = 212736


def build_program(debug=False):
    nc = bass.Bass("TRN2", target_bir_lowering=False)
    dt_in = lambda name, shape, dt=F32: nc.dram_tensor(name, shape, dt, kind="ExternalInput")
    x_ext = dt_in("x_ext", [NCOL, D])
    tab_d = dt_in("rot_tab", [2, 32, NCOL])
    maskp_d = dt_in("maskp", [128, 4 * 256], BF)
    maskx_d = dt_in("maskx", [128, 4], BF)
    invcnt_d = dt_in("invcnt", [4, NQA])
    flags_d = dt_in("flags", [128, 1])
    ident_d = dt_in("ident", [128, 128], BF)
    pm_d = dt_in("permm", [128, 32], BF)
    g1_d = dt_in("norm_mix_g", [D])
    g2_d = dt_in("norm_ffn_g", [D])
    g3_d = dt_in("norm_final_g", [D])
    w_in = dt_in("w_in", [D, 5120])
    w_ao = dt_in("w_attn_out", [512, D])
    pool_w = dt_in("pool_w", [4, 128, 128])
    pool_scale = dt_in("pool_scale", [512])
    w_po = dt_in("w_pool_out", [512, D])
    w_gate = dt_in("w_gate", [D, 2 * D])
    w_out = dt_in("w_out", [D, D])
    w_up = dt_in("w_up", [D, 2 * DFF])
    conv_w = dt_in("conv_w", [3, DFF])
    conv_b = dt_in("conv_b", [DFF])
    w_down = dt_in("w_down", [DFF, D])
    out_d = nc.dram_tensor("out", [OWN, D], F32, kind="ExternalOutput")
    uTs = nc.dram_tensor("uT_s", [128, KC * NCOL], BF)
    NQP = 2080
    ZWP = 1056
    zTs = nc.dram_tensor("zT_s", [128, KC * NQP], BF)
    h_s = nc.dram_tensor("h_s", [NQ, D], F32)
    dbg = {}
    if debug:
        dbg['attnT'] = nc.dram_tensor("dbg_attnT", [128, 4 * NQ], BF, kind="ExternalOutput")
        dbg['pmT'] = nc.dram_tensor("dbg_pmT", [128, 4 * NQ], BF, kind="ExternalOutput")
        dbg['mixT'] = nc.dram_tensor("dbg_mixT", [128, KC * NQ], BF, kind="ExternalOutput")

    S = Sched()
    st = ExitStack()
    arena16 = st.enter_context(nc.sbuf_tensor("arena", [128, ARENA_BYTES // 2], BF))
    arena32 = arena16.bitcast(F32)
    psum = [st.enter_context(nc.psum_tensor("psb%d" % i, [128, 1024], F32)) for i in range(4)]
    psum16 = [p.bitcast(BF) for p in psum]

    def PS(bank, dt=F32):
        if dt == F32:
            return Reg(psum[bank // 2], 1024, (bank % 2) * 512)
        return Reg(psum16[bank // 2], 2048, (bank % 2) * 1024)

    bank_ctr = [0]
    bank_pool = [list(range(8))]

    def nbank():
        bp = bank_pool[0]
        b = bp[bank_ctr[0] % len(bp)]
        bank_ctr[0] += 1
        return b

    abytes = [0]

    def alloc(nbytes, dt):
        off = (abytes[0] + 63) // 64 * 64
        abytes[0] = off + nbytes
        assert abytes[0] <= ARENA_BYTES, ("arena overflow", abytes[0])
        if dt == F32:
            return Reg(arena32, ARENA_BYTES // 4, off // 4)
        return Reg(arena16, ARENA_BYTES // 2, off // 2)

    def barrier():
        S.add('dve', lambda e: e.memset(bar_t.ap(0, [[1, 1]]), 0.0), writes=['PHASE'])

    bar_t = alloc(4, F32)
    idt = alloc(256, BF)
    pmm = alloc(64, BF)
    ones = alloc(256, BF)
    maskp = alloc(4 * 512, BF)
    maskx = alloc(8, BF)
    epst = alloc(4, F32)
    flags = alloc(4, F32)
    pscale = alloc(16, F32)
    convw = alloc(3 * 48 * 4, F32)
    convb = alloc(48 * 4, F32)
    ssq = alloc(4 * 8, F32)
    rstd = alloc(4 * 8, F32)
    const_mark = abytes[0]
    attnT = alloc(4 * NQ * 2, BF)
    pmT = alloc(4 * NQ * 2, BF)
    base_mark = abytes[0]

    def dma_in(eng, dst_ap, src_ap, wkeys, rkeys=()):
        S.add(eng, lambda e: e.dma_start(out=dst_ap, in_=src_ap), reads=rkeys, writes=wkeys, dma=True)

    dma_in('sp', idt.ap(0, [[1, 128]]), ident_d.ap(), ['idt'])
    dma_in('sp', pmm.ap(0, [[1, 32]]), pm_d.ap(), ['pmm'])
    dma_in('sp', maskp.ap(0, [[1, 1024]]), maskp_d.ap(), ['maskp'])
    dma_in('sp', maskx.ap(0, [[1, 4]]), maskx_d.ap(), ['maskx'])
    dma_in('sp', flags.ap(0, [[1, 1]]), flags_d.ap(), ['flags'])
    def late_param_loads_pool():
        S.add('sp', lambda e: e.dma_start(out=pscale.ap(0, [[1, 4]]), in_=bass.AP(pool_scale, 0, [[1, 128], [128, 4]]),
                                          allow_slow_non_contiguous=True), writes=['pscale'], dma=True)
        for pg in range(4):
            S.add('sp', lambda e, pg=pg: e.dma_start(out=icnt4[pg].ap(0, [[1, NQA]]),
                                                     in_=bass.AP(invcnt_d, pg * NQA, [[0, 128], [1, NQA]])),
                  writes=[('icnt', pg)], dma=True)

    def late_param_loads_conv():
        S.add('sp', lambda e: e.dma_start(out=convw.ap(0, [[48, 3], [1, 48]]),
                                          in_=bass.AP(conv_w, 0, [[1, 128], [DFF, 3], [128, 48]]), allow_slow_non_contiguous=True),
              writes=['convw'], dma=True)
        S.add('sp', lambda e: e.dma_start(out=convb.ap(0, [[1, 48]]), in_=bass.AP(conv_b, 0, [[1, 128], [128, 48]]),
                                          allow_slow_non_contiguous=True), writes=['convb'], dma=True)
    S.add('dve', lambda e: e.memset(ones.ap(0, [[1, 128]]), 1.0), writes=['ones'])
    S.add('dve', lambda e: e.memset(epst.ap(0, [[1, 1]]), EPS), writes=['epst'])

    def wload(dst, doff, W, ld, col0, ncols, nk, key, row0=0):
        S.add('pool', lambda e: e.dma_start(out=dst.ap(doff, [[ncols, nk], [1, ncols]]),
                                            in_=bass.AP(W, row0 * ld + col0, [[ld, 128], [128 * ld, nk], [1, ncols]])),
              writes=[key], dma=True)

    def bcast_load(dst, vec, key):
        S.add('sp', lambda e: e.dma_start(out=dst.ap(0, [[1, D]]), in_=bass.AP(vec, 0, [[0, 128], [1, D]])),
              writes=[key], dma=True)

    def rms_rstd(src, srckey, junk, si, extra_flag=False, npart=128):
        S.add('act', lambda e: e.activation(out=junk.ap(0, [[1, D]], n=npart), in_=src.ap(0, [[1, D]], n=npart),
                                            func=AF.Square, scale=float(D) ** -0.5,
                                            accum_out=ssq.ap(si, [[1, 1]], n=npart)),
              reads=[srckey], writes=['junk', ('ssq', si)])
        S.add('act', lambda e: e.activation(out=rstd.ap(si, [[1, 1]], n=npart), in_=ssq.ap(si, [[1, 1]], n=npart),
                                            func=AF.Sqrt, bias=epst.ap(0, [[1, 1]], n=npart), scale=1.0),
              reads=[('ssq', si), 'epst'], writes=[('rstd', si)])
        S.add('dve', lambda e: e.reciprocal(out=rstd.ap(si, [[1, 1]], n=npart), in_=rstd.ap(si, [[1, 1]], n=npart)),
              reads=[('rstd', si)], writes=[('rstd', si)])
        if extra_flag:
            S.add('dve', lambda e: e.tensor_tensor(out=rstd.ap(si, [[1, 1]], n=npart), in0=rstd.ap(si, [[1, 1]], n=npart),
                                                   in1=flags.ap(0, [[1, 1]], n=npart), op=ALU.mult),
                  reads=[('rstd', si), 'flags'], writes=[('rstd', si)])

    def transpose16(src, srckey, npart, bankpair):
        pr = Reg(psum16[bankpair], 2048, 0)
        for j in range(KC):
            S.add('pe', lambda e, j=j: e.transpose(out=pr.ap(j * 128, [[1, npart]]),
                                                   in_=src.ap(j * 128, [[1, 128]], n=npart),
                                                   identity=idt.ap(0, [[1, npart]], n=npart)),
                  reads=[srckey, 'idt'], writes=[('psp', bankpair)])
        return pr

    markA = abytes[0]
    gbc = alloc(D * 4, F32)
    junk = alloc(D * 2, BF)
    XB = 3
    xt = [alloc(D * 4, F32) for _ in range(XB)]
    ub = [alloc(D * 2, BF) for _ in range(2)]
    ustage = [alloc(KC * TW * 2, BF) for _ in range(2)]
    save_ab = abytes[0]
    assert abytes[0] <= 100 * 1024
    abytes[0] = 100 * 1024
    Sa = alloc(NQA * 4, F32)
    Sb = alloc(NQA * 4, F32)
    icnt4 = [alloc(NQA * 4, F32) for _ in range(4)]
    pooled = alloc(NQ * 2, BF)
    pw_t = alloc(4 * 128 * 2, BF)
    pwts = alloc(4 * KC * 128 * 2, BF)
    p32 = alloc(4 * NQA * 4, F32)
    abytes[0] = save_ab
    bank_pool[0] = [4, 5, 6, 7]
    for pg in range(4):
        wload(pwts, pg * KC * 128, w_in, 5120, 4608 + pg * 128, 128, KC, ('pw', pg))
    S.add('pool', lambda e: e.dma_start(out=pw_t.ap(0, [[128, 4], [1, 128]]),
                                        in_=bass.AP(pool_w, 0, [[128, 128], [128 * 128, 4], [1, 128]])),
          writes=['pw_t'], dma=True)
    bcast_load(gbc, g1_d, 'gbc')
    NTA = NCOL // 128

    def pool_tail():
        for pg, wd in enumerate((2, 4, 8, 16)):
            pk = [('p32', pg, i) for i in range(NT)]
            icnt = icnt4[pg]
            src, srck = p32, pk
            soff = pg * NQA
            sh = 1
            v0 = 0
            bufs = [(Sa, 'Sa'), (Sb, 'Sb')]
            bi_ = 0
            while sh < wd:
                dst, dk = bufs[bi_]
                bi_ ^= 1
                nn = NQA - v0 - sh
                S.add('dve', lambda e, dst=dst, src=src, soff=soff, sh=sh, v0=v0, nn=nn: e.tensor_tensor(
                    out=dst.ap(v0 + sh, [[1, nn]]), in0=src.ap(soff + v0 + sh, [[1, nn]]), in1=src.ap(soff + v0, [[1, nn]]),
                    op=ALU.add), reads=list(srck), writes=[dk])
                src, srck, soff = dst, [dk], 0
                v0 += sh
                sh *= 2
            o = 15 + wd // 2 - 1
            assert o >= v0
            other, ok_ = bufs[bi_]
            S.add('dve', lambda e, src=src, o=o, other=other, icnt=icnt: e.tensor_tensor(
                out=other.ap(0, [[1, NQ]]), in0=src.ap(o, [[1, NQ]]), in1=icnt.ap(15, [[1, NQ]]), op=ALU.mult),
                reads=list(srck) + [('icnt', pg)], writes=[ok_])
            S.add('dve', lambda e, other=other, pg=pg: e.tensor_tensor(
                out=pooled.ap(0, [[1, NQ]]), in0=other.ap(0, [[1, NQ]]), in1=p32.ap(pg * NQA + 15, [[1, NQ]]), op=ALU.subtract),
                reads=[ok_] + pk, writes=['pooled'])
            for tq in range(5):
                bank = nbank()
                S.add('pe', lambda e, tq=tq, bank=bank, pg=pg: e.matmul(PS(bank).ap(0, [[1, 410]]), lhsT=pw_t.ap(pg * 128, [[1, 128]]),
                                                                        rhs=pooled.ap(tq * 410, [[1, 410]]), start=True, stop=True),
                      reads=['pw_t', 'pooled'], writes=[('ps', bank)])
                S.add('act', lambda e, tq=tq, bank=bank, pg=pg: e.activation(
                    out=pmT.ap(pg * NQ + tq * 410, [[1, 410]]), in_=PS(bank).ap(0, [[1, 410]]), func=AF.Copy,
                    scale=pscale.ap(pg, [[1, 1]])), reads=[('ps', bank), 'pscale'], writes=[('pmT', pg)])

    pendA = []

    def pool_proj(ti, sbuf_i):
        c0 = max(ti * TW, QA0)
        c1 = min(ti * TW + TW, QA0 + NQA, NCOL)
        if c1 <= c0:
            return
        a0, n = c0 - ti * TW, c1 - c0
        for pg in range(4):
            bank = nbank()
            for k in range(KC):
                S.add('pe', lambda e, k=k, bank=bank, pg=pg: e.matmul(
                    PS(bank).ap(0, [[1, n]]), lhsT=pwts.ap(pg * KC * 128 + k * 128, [[1, 128]]),
                    rhs=ustage[sbuf_i].ap(k * TW + a0, [[1, n]]), start=(k == 0), stop=(k == KC - 1)),
                    reads=[('pw', pg), ('ustage', sbuf_i, 0), ('ustage', sbuf_i, 1)], writes=[('ps', bank)])
            def pevac(pg=pg, bank=bank):
                if pg % 2 == 0:
                    S.add('act', lambda e: e.activation(
                        out=p32.ap(pg * NQA + c0 - QA0, [[1, n]]), in_=PS(bank).ap(0, [[1, n]]), func=AF.Copy),
                        reads=[('ps', bank)], writes=[('p32', pg, ti)])
                else:
                    S.add('dve', lambda e: e.tensor_copy(
                        out=p32.ap(pg * NQA + c0 - QA0, [[1, n]]), in_=PS(bank).ap(0, [[1, n]])),
                        reads=[('ps', bank)], writes=[('p32', pg, ti)])
            pendA.append(pevac)
        if c1 == QA0 + NQA:
            pendA.append(pool_tail)

    def a_load(tt):
        xb = tt % XB
        dma_in('sp', xt[xb].ap(0, [[1, D]]), bass.AP(x_ext, tt * 128 * D, [[D, 128], [1, D]]), [('xt', xb)])

    def a_front(tt):
        b2 = tt % 2
        xb = tt % XB
        rms_rstd(xt[xb], ('xt', xb), junk, b2)
        S.add('dve', lambda e: e.scalar_tensor_tensor(out=ub[b2].ap(0, [[1, D]]), in0=xt[xb].ap(0, [[1, D]]),
                                                      scalar=rstd.ap(b2, [[1, 1]]), in1=gbc.ap(0, [[1, D]]),
                                                      op0=ALU.mult, op1=ALU.mult),
              reads=[('xt', xb), ('rstd', b2), 'gbc'], writes=[('ub', b2)])
        return transpose16(ub[b2], ('ub', b2), 128, b2)

    def a_back(tt, pr):
        b2 = tt % 2
        ti = tt // 2
        half = tt % 2
        sbuf_i = ti % 2
        if tt % 2 == 0:
            S.add('act', lambda e: e.activation(
                out=ustage[sbuf_i].ap(half * 128, [[TW, KC], [1, 128]]), in_=pr.ap(0, [[128, KC], [1, 128]]), func=AF.Copy),
                reads=[('psp', b2)], writes=[('ustage', sbuf_i, half)])
        else:
            S.add('dve', lambda e: e.tensor_copy(
                out=ustage[sbuf_i].ap(half * 128, [[TW, KC], [1, 128]]), in_=pr.ap(0, [[128, KC], [1, 128]])),
                reads=[('psp', b2)], writes=[('ustage', sbuf_i, half)])
        if half == 1 or tt == NTA - 1:
            wcols = 128 * (half + 1)
            S.add('sp', lambda e: e.dma_start(
                out=bass.AP(uTs, ti * TW, [[KC * NCOL, 128], [NCOL, KC], [1, wcols]]),
                in_=ustage[sbuf_i].ap(0, [[TW, KC], [1, wcols]])),
                reads=[('ustage', sbuf_i, 0), ('ustage', sbuf_i, 1)], writes=[('uTs', ti)], dma=True)
            pool_proj(ti, sbuf_i)

    def flushA():
        while pendA:
            pendA.pop(0)()

    a_load(0)
    a_load(1)
    prevA = None
    for tt in range(NTA):
        if tt + 2 < NTA:
            a_load(tt + 2)
        if tt == 3:
            late_param_loads_pool()
        prA = a_front(tt)
        flushA()
        if prevA is not None:
            a_back(*prevA)
        prevA = (tt, prA)
    flushA()
    a_back(*prevA)
    flushA()
    barrier()

    abytes[0] = markA
    bank_pool[0] = list(range(8))
    NWR = 12
    wring = alloc(NWR * KC * 128 * 2, BF)
    wctr = [0]

    def wslot():
        i = wctr[0] % NWR
        wctr[0] += 1
        return i

    utile = [alloc(KC * TW * 2, BF) for _ in range(3)]
    assert abytes[0] >= markA + KC * NQ * 2
    uTm_pre = Reg(arena16, ARENA_BYTES // 2, ((markA + 63) // 64 * 64) // 2)
    tabt = [alloc(2 * TW * 4, F32) for _ in range(3)]
    qraw = [alloc(TW * 2, BF) for _ in range(5)]
    rtmpA = alloc(TW * 4, F32)
    rtmpB = alloc(TW * 4, F32)
    markB = abytes[0]
    utc = [0]

    def load_utile(i):
        w = min(TW, NCOL - i * TW)
        bi = utc[0] % 3
        utc[0] += 1
        S.add('sp', lambda e: e.dma_start(out=utile[bi].ap(0, [[TW, KC], [1, w]]),
                                          in_=bass.AP(uTs, i * TW, [[KC * NCOL, 128], [NCOL, KC], [1, w]])),
              reads=[('uTs', i)], writes=[('utile', bi)], dma=True)
        S.add('sp', lambda e: e.dma_start(out=tabt[bi].ap(0, [[TW, 2], [1, w]], n=32),
                                          in_=bass.AP(tab_d, i * TW, [[NCOL, 32], [32 * NCOL, 2], [1, w]])),
              writes=[('tabt', bi)], dma=True)
        return bi, w

    def proj_chunk(bi, a0, n, wi):
        bank = nbank()
        pr = PS(bank)
        for k in range(KC):
            S.add('pe', lambda e, k=k: e.matmul(pr.ap(0, [[1, n]]), lhsT=wring.ap(wi * KC * 128 + k * 128, [[1, 128]]),
                                                rhs=utile[bi].ap(k * TW + a0, [[1, n]]), start=(k == 0), stop=(k == KC - 1)),
                  reads=[('w', wi), ('utile', bi)], writes=[('ps', bank)])
        return bank

    abytes[0] = markB
    LQ = {d: NQA // d for d in DILS}
    LK = {d: OWN // d + 130 for d in DILS}
    K0 = {d: HL - 65 * d for d in DILS}
    NK = {d: OWN + 130 * d for d in DILS}
    qT = {d: alloc(NQA * 2, BF) for d in DILS}
    kT = {d: alloc(NK[d] * 2, BF) for d in DILS}
    vT = {d: alloc(NK[d] * 2, BF) for d in DILS}
    NVC = 20
    Vc = alloc(NVC * 128 * 2, BF)
    acc = alloc(2 * NQA * 4, F32)
    pT = [alloc(256 * 2, BF) for _ in range(6)]
    rden = alloc(NQ * 4, F32)
    SCALE = 128.0 ** -0.5

    rtA = [rtmpA] + [alloc(TW * 4, F32) for _ in range(4)]
    rtB = [rtmpB] + [alloc(TW * 4, F32) for _ in range(4)]
    rctr = [0]
    pend = []

    def flush(keep=0):
        while len(pend) > keep:
            pend.pop(0)()

    def rot_evac(bank, n, bi, a0, dst, d, j0, kind_key):
        nj = n // d
        L = LK[d] if dst is kT[d] else LQ[d]
        qi = rctr[0] % 5
        rctr[0] += 1
        S.add('dve', lambda e: e.tensor_copy(out=qraw[qi].ap(0, [[1, n]]), in_=PS(bank).ap(0, [[1, n]])),
              reads=[('ps', bank)], writes=[('qraw', qi)])

        def part2():
            b2 = nbank()
            S.add('pe', lambda e: e.matmul(PS(b2).ap(0, [[1, n]], n=32), lhsT=pmm.ap(0, [[1, 32]]), rhs=qraw[qi].ap(0, [[1, n]]),
                                           start=True, stop=True), reads=['pmm', ('qraw', qi)], writes=[('ps', b2)])
            S.add('act', lambda e: e.activation(out=dst.ap(j0, [[L, d], [1, nj]]),
                                                in_=qraw[qi].ap(0, [[1, d], [d, nj]]), func=AF.Copy),
                  reads=[('qraw', qi)], writes=[kind_key])
            S.add('dve', lambda e: e.tensor_tensor(out=rtA[qi].ap(0, [[1, n]], n=32), in0=qraw[qi].ap(0, [[1, n]], n=32),
                                                   in1=tabt[bi].ap(a0, [[1, n]], n=32), op=ALU.mult),
                  reads=[('qraw', qi), ('tabt', bi)], writes=[('rtA', qi)])
            S.add('dve', lambda e: e.tensor_tensor(out=rtB[qi].ap(0, [[1, n]], n=32), in0=PS(b2).ap(0, [[1, n]], n=32),
                                                   in1=tabt[bi].ap(TW + a0, [[1, n]], n=32), op=ALU.mult),
                  reads=[('ps', b2), ('tabt', bi)], writes=[('rtB', qi)])
            S.add('dve', lambda e: e.tensor_tensor(out=dst.ap(j0, [[L, d], [1, nj]], n=32), in0=rtA[qi].ap(0, [[1, d], [d, nj]], n=32),
                                                   in1=rtB[qi].ap(0, [[1, d], [d, nj]], n=32), op=ALU.add),
                  reads=[('rtA', qi), ('rtB', qi)], writes=[kind_key])
        pend.append(part2)

    kT_keys = {d: [] for d in DILS}
    vT_keys = {d: [] for d in DILS}
    qT_keys = {d: [] for d in DILS}
    def load_slot_weights(s_):
        wm = {}
        order = [(2, 'k'), (2, 'v'), (1, 'k'), (1, 'v'), (0, 'k'), (0, 'v'), (2, 'q'), (1, 'q'), (0, 'q')]
        cb = {'k': 1536, 'v': 3072, 'q': 0}
        for g, kind in order:
            hh = g * 4 + s_
            wi = wslot()
            wload(wring, wi * KC * 128, w_in, 5120, cb[kind] + hh * 128, 128, KC, ('w', wi))
            wm[(g, kind)] = wi
        return wm

    wmap_next = load_slot_weights(0)
    for s in range(4):
        wmap = wmap_next
        for d in DILS:
            kT_keys[d], vT_keys[d], qT_keys[d] = [], [], []
        for i in range(NT):
            t0, t1 = i * TW, min(i * TW + TW, NCOL)
            todo = []
            for g, d in enumerate(DILS):
                c0, c1 = max(t0, K0[d]), min(t1, K0[d] + NK[d])
                if c1 > c0:
                    todo.append((g, d, 'k', c0, c1))
                    todo.append((g, d, 'v', c0, c1))
                c0, c1 = max(t0, QA0), min(t1, QA0 + NQA)
                if c1 > c0:
                    todo.append((g, d, 'q', c0, c1))
            if not todo:
                continue
            bi, w = load_utile(i)
            for (g, d, kind, c0, c1) in todo:
                a0, n = c0 - t0, c1 - c0
                bank = proj_chunk(bi, a0, n, wmap[(g, kind)])
                flush(keep=3)
                if kind == 'v':
                    j0 = (c0 - K0[d]) // d
                    nj = n // d
                    vT_keys[d].append(('vT', d, i))
                    S.add('act', lambda e, bank=bank, d=d, j0=j0, nj=nj, n=n: e.activation(
                        out=vT[d].ap(j0, [[LK[d], d], [1, nj]]), in_=PS(bank).ap(0, [[1, d], [d, nj]]), func=AF.Copy),
                        reads=[('ps', bank)], writes=[('vT', d, i)])
                elif kind == 'k':
                    kT_keys[d].append(('kT', d, i))
                    rot_evac(bank, n, bi, a0, kT[d], d, (c0 - K0[d]) // d, ('kT', d, i))
                else:
                    qT_keys[d].append(('qT', d, i))
                    rot_evac(bank, n, bi, a0, qT[d], d, (c0 - QA0) // d, ('qT', d, i))
        flush()
        if s + 1 < 4:
            wmap_next = load_slot_weights(s + 1)
        else:
            alias = [('w', i_) for i_ in range(NWR)] + [('utile', b_) for b_ in range(3)] + [('tabt', b_) for b_ in range(3)]
            for k in range(KC):
                S.add('sp', lambda e, k=k: e.dma_start(out=uTm_pre.ap(k * NQ, [[1, NQ]]),
                                                       in_=bass.AP(uTs, k * NCOL + HL - 1, [[KC * NCOL, 128], [1, NQ]])),
                      reads=[('uTs', i_) for i_ in range(NT)], writes=[('uTm', k)] + (alias if k == 0 else []), dma=True)

        vctr = [0]

        def vchunk(d, r, jk0):
            slot = vctr[0] % NVC
            vctr[0] += 1
            bank = nbank()
            pr = PS(bank, BF)
            S.add('pe', lambda e: e.transpose(out=pr.ap(0, [[1, 128]]), in_=vT[d].ap(r * LK[d] + jk0, [[1, 128]]),
                                              identity=idt.ap(0, [[1, 128]])),
                  reads=list(vT_keys[d]) + ['idt'], writes=[('ps', bank)])
            S.add('act', lambda e: e.activation(out=Vc.ap(slot * 128, [[1, 128]]), in_=pr.ap(0, [[1, 128]]), func=AF.Copy),
                  reads=[('ps', bank)], writes=[('Vc', slot)])
            return slot

        blk = [0]

        def attn_block(d, r, jq0, N, chunks, mask_ap, maskkey, acc_off, acc_step, first, akeys):
            bs = nbank()
            for h, (jk0, vs) in enumerate(chunks):
                S.add('pe', lambda e, h=h, jk0=jk0: e.matmul(PS(bs).ap(h * N, [[1, N]]), lhsT=kT[d].ap(r * LK[d] + jk0, [[1, 128]]),
                                                             rhs=qT[d].ap(r * LQ[d] + jq0, [[1, N]]), start=True, stop=True),
                      reads=list(kT_keys[d]) + list(qT_keys[d]), writes=[('ps', bs)])
            pi = blk[0] % 6
            blk[0] += 1
            flush(keep=4)
            S.add('act', lambda e: e.activation(out=pT[pi].ap(0, [[1, 2 * N]]), in_=PS(bs).ap(0, [[1, 2 * N]]), func=AF.Exp, scale=SCALE),
                  reads=[('ps', bs)], writes=[('pT', pi)])
            S.add('dve', lambda e: e.tensor_tensor(out=pT[pi].ap(0, [[1, 2 * N]]), in0=pT[pi].ap(0, [[1, 2 * N]]), in1=mask_ap, op=ALU.mult),
                  reads=[('pT', pi), maskkey], writes=[('pT', pi)])

            def stage2():
                bn = nbank()
                for h, (jk0, vs) in enumerate(chunks):
                    S.add('pe', lambda e, h=h, vs=vs: e.matmul(PS(bn).ap(0, [[1, N]]), lhsT=Vc.ap(vs * 128, [[1, 128]]),
                                                               rhs=pT[pi].ap(h * N, [[1, N]]), start=(h == 0), stop=(h == 1)),
                          reads=[('Vc', vs), ('pT', pi)], writes=[('ps', bn)])
                for h in range(2):
                    S.add('pe', lambda e, h=h: e.matmul(PS(bn).ap(N, [[1, N]]), lhsT=ones.ap(0, [[1, 128]]),
                                                        rhs=pT[pi].ap(h * N, [[1, N]]), start=(h == 0), stop=(h == 1)),
                          reads=['ones', ('pT', pi)], writes=[('ps', bn)])
                if first:
                    S.add('dve', lambda e: e.tensor_copy(out=acc.ap(acc_off, [[NQA, 2], [acc_step, N]]), in_=PS(bn).ap(0, [[N, 2], [1, N]])),
                          reads=[('ps', bn)], writes=akeys)
                else:
                    S.add('dve', lambda e: e.tensor_tensor(out=acc.ap(acc_off, [[NQA, 2], [acc_step, N]]), in0=PS(bn).ap(0, [[N, 2], [1, N]]),
                                                           in1=acc.ap(acc_off, [[NQA, 2], [acc_step, N]]), op=ALU.add),
                          reads=[('ps', bn)] + akeys, writes=akeys)
            pend.append(stage2)

        for g, d in enumerate(DILS):
            L = OWN // d
            nb = L // 128
            lq0 = 16 // d
            for r in range(d):
                vs = [vchunk(d, r, 128 * c + 1) for c in range(nb + 1)]
                for b in range(nb):
                    mp = (3 if nb == 1 else 0) if b == 0 else (2 if b == nb - 1 else 1)
                    akeys = [('acc', d * b + jj) for jj in range(d)]
                    attn_block(d, r, lq0 + 128 * b, 128, [(128 * b + 1, vs[b]), (128 * (b + 1) + 1, vs[b + 1])],
                               maskp.ap(mp * 256, [[1, 256]]), 'maskp', 16 + r + d * 128 * b, d, g == 0, akeys)
                if r == 0:
                    vx = vchunk(d, 0, L + 2)
                    attn_block(d, 0, lq0 + L, 1, [(L + 1, vs[nb]), (L + 2, vx)],
                               maskx.ap(2, [[1, 2]]), 'maskx', 16 + OWN, 1, g == 0, [('acc', 'R')])
                if r == d - 1:
                    vx1 = vchunk(d, r, 0)
                    vx2 = vchunk(d, r, 128)
                    attn_block(d, r, lq0 - 1, 1, [(0, vx1), (128, vx2)],
                               maskx.ap(0, [[1, 2]]), 'maskx', 15, 1, g == 0, [('acc', 'L')])
        flush()
        allacc = [('acc', jj) for jj in range(16)] + [('acc', 'L'), ('acc', 'R')]
        S.add('act', lambda e: e.activation(out=rden.ap(0, [[1, NQ]]), in_=acc.ap(NQA + 15, [[1, NQ]]), func=AF.Ln),
              reads=allacc, writes=['rden'])
        S.add('act', lambda e: e.activation(out=rden.ap(0, [[1, NQ]]), in_=rden.ap(0, [[1, NQ]]), func=AF.Exp, scale=-1.0),
              reads=['rden'], writes=['rden'])
        S.add('dve', lambda e, s=s: e.tensor_tensor(out=attnT.ap(s * NQ, [[1, NQ]]), in0=acc.ap(15, [[1, NQ]]),
                                                    in1=rden.ap(0, [[1, NQ]]), op=ALU.mult),
              reads=allacc + ['rden'], writes=[('attnT', s)])
    barrier()
    if debug:
        S.add('sp', lambda e: e.dma_start(out=dbg['attnT'].ap(), in_=attnT.ap(0, [[1, 4 * NQ]])),
              reads=[('attnT', s) for s in range(4)], writes=['dbg1'], dma=True)
        S.add('sp', lambda e: e.dma_start(out=dbg['pmT'].ap(), in_=pmT.ap(0, [[1, 4 * NQ]])),
              reads=[('pmT', s) for s in range(4)], writes=['dbg2'], dma=True)

    abytes[0] = base_mark
    MIXOFF = (ARENA_BYTES - KC * NQ * 2) // 64 * 64
    mixT = Reg(arena16, ARENA_BYTES // 2, MIXOFF // 2)
    uTm = alloc(KC * NQ * 2, BF)
    NWC = 3
    wgr = alloc(NWC * (2 * KC * 128 + 2 * 4 * 128) * 2, BF)
    WGS = 2 * KC * 128 + 2 * 4 * 128
    gst = [alloc(410 * 4, F32) for _ in range(4)]
    assert abytes[0] <= MIXOFF, abytes[0]
    CQ0 = HL - 1
    assert uTm.base == uTm_pre.base and base_mark == markA
    late_param_loads_conv()
    for f in range(KC):
        wi = f % NWC
        wb = wi * WGS
        wload(wgr, wb, w_gate, 2 * D, f * 128, 128, KC, ('wg', wi, 0))
        wload(wgr, wb + KC * 128, w_gate, 2 * D, D + f * 128, 128, KC, ('wg', wi, 1))
        wload(wgr, wb + 2 * KC * 128, w_ao, D, f * 128, 128, 4, ('wg', wi, 2))
        wload(wgr, wb + 2 * KC * 128 + 512, w_po, D, f * 128, 128, 4, ('wg', wi, 3))
        for tq in range(5):
            q0 = tq * 410
            banks = []
            for part in range(2):
                bank = nbank()
                for k in range(KC):
                    S.add('pe', lambda e, k=k, bank=bank, part=part, wb=wb, q0=q0: e.matmul(
                        PS(bank).ap(0, [[1, 410]]), lhsT=wgr.ap(wb + part * KC * 128 + k * 128, [[1, 128]]),
                        rhs=uTm.ap(k * NQ + q0, [[1, 410]]), start=(k == 0), stop=(k == KC - 1)),
                        reads=[('wg', wi, part), ('uTm', k)], writes=[('ps', bank)])
                S.add('act', lambda e, bank=bank, part=part: e.activation(out=gst[part].ap(0, [[1, 410]]), in_=PS(bank).ap(0, [[1, 410]]),
                                                                          func=AF.Sigmoid),
                      reads=[('ps', bank)], writes=[('gst', part)])
                banks.append(bank)
            for part, (srcT, skey) in enumerate(((attnT, 'attnT'), (pmT, 'pmT'))):
                bank = nbank()
                for k in range(4):
                    S.add('pe', lambda e, k=k, bank=bank, part=part, srcT=srcT, wb=wb, q0=q0: e.matmul(
                        PS(bank).ap(0, [[1, 410]]), lhsT=wgr.ap(wb + 2 * KC * 128 + part * 512 + k * 128, [[1, 128]]),
                        rhs=srcT.ap(k * NQ + q0, [[1, 410]]), start=(k == 0), stop=(k == 3)),
                        reads=[('wg', wi, 2 + part)] + [(skey, s_) for s_ in range(4)], writes=[('ps', bank)])
                S.add('dve', lambda e, bank=bank, part=part: e.tensor_tensor(
                    out=gst[2 + part].ap(0, [[1, 410]]), in0=PS(bank).ap(0, [[1, 410]]), in1=gst[part].ap(0, [[1, 410]]), op=ALU.mult),
                    reads=[('ps', bank), ('gst', part)], writes=[('gst', 2 + part)])
            S.add('dve', lambda e, f=f, q0=q0: e.tensor_tensor(out=mixT.ap(f * NQ + q0, [[1, 410]]), in0=gst[2].ap(0, [[1, 410]]),
                                                               in1=gst[3].ap(0, [[1, 410]]), op=ALU.add),
                  reads=[('gst', 2), ('gst', 3)], writes=[('mixT', f)])
    barrier()
    if debug:
        S.add('sp', lambda e: e.dma_start(out=dbg['mixT'].ap(), in_=mixT.ap(0, [[1, KC * NQ]])),
              reads=[('mixT', f) for f in range(KC)], writes=['dbg3'], dma=True)

    abytes[0] = const_mark
    bank_pool[0] = [0, 1, 2, 3]
    wo = alloc(KC * D * 2, BF)
    gbc2 = alloc(D * 4, F32)
    junk2 = alloc(D * 2, BF)
    xt2 = [alloc(D * 4, F32) for _ in range(3)]
    ht = [alloc(D * 4, F32) for _ in range(2)]
    zb = [alloc(D * 2, BF) for _ in range(2)]
    zst = [alloc(KC * 128 * 2, BF) for _ in range(2)]
    assert abytes[0] <= MIXOFF, abytes[0]
    for fb in range(4):
        S.add('pool', lambda e, fb=fb: e.dma_start(out=wo.ap(fb * 512, [[D, KC], [1, 512]]),
                                                   in_=bass.AP(w_out, fb * 512, [[D, 128], [128 * D, KC], [1, 512]])),
              writes=[('wo', fb)], dma=True)
    bcast_load(gbc2, g2_d, 'gbc2')
    mixkeys = [('mixT', f) for f in range(KC)]

    def d_xload(m, pos_):
        xb = pos_ % 3
        if m < 16:
            dma_in('sp', xt2[xb].ap(0, [[1, D]]), bass.AP(x_ext, (HL + 128 * m) * D, [[D, 128], [1, D]]), [('xt2', xb)])
        else:
            dma_in('sp', xt2[xb].ap(0, [[1, D]], n=2), bass.AP(x_ext, (HL - 1) * D, [[(OWN + 1) * D, 2], [1, D]]), [('xt2', xb)])

    def d_front(m, pos_):
        b2 = pos_ % 2
        xb = pos_ % 3
        if pos_ + 2 < 17:
            d_xload(d_order[pos_ + 2], pos_ + 2)
        if m < 16:
            npart = 128
            lcols = lambda k: mixT.ap(k * NQ + 1 + 128 * m, [[1, 128]])
        else:
            npart = 2
            lcols = lambda k: mixT.ap(k * NQ, [[NQ - 1, 2]])
        for fb in range(4):
            bank = nbank()
            for k in range(KC):
                S.add('pe', lambda e, k=k, bank=bank, fb=fb: e.matmul(
                    PS(bank).ap(0, [[1, 512]], n=npart), lhsT=lcols(k), rhs=wo.ap(k * D + fb * 512, [[1, 512]]),
                    start=(k == 0), stop=(k == KC - 1)), reads=mixkeys + [('wo', fb)], writes=[('ps', bank)])
            S.add('dve', lambda e, bank=bank, fb=fb: e.tensor_tensor(
                out=ht[b2].ap(fb * 512, [[1, 512]], n=npart), in0=PS(bank).ap(0, [[1, 512]], n=npart),
                in1=xt2[xb].ap(fb * 512, [[1, 512]], n=npart), op=ALU.add),
                reads=[('ps', bank), ('xt2', xb)], writes=[('ht', b2, fb)])
        hkeys = [('ht', b2, fb) for fb in range(4)]
        if m < 16:
            S.add('sp', lambda e: e.dma_start(out=bass.AP(h_s, (1 + 128 * m) * D, [[D, 128], [1, D]]),
                                              in_=ht[b2].ap(0, [[1, D]])),
                  reads=hkeys, writes=[('h_s', m)], dma=True)
        S.add('act', lambda e: e.activation(out=junk2.ap(0, [[1, D]], n=npart), in_=ht[b2].ap(0, [[1, D]], n=npart),
                                            func=AF.Square, scale=float(D) ** -0.5,
                                            accum_out=ssq.ap(b2, [[1, 1]], n=npart)),
              reads=hkeys, writes=['junk2', ('ssq', b2)])
        S.add('act', lambda e: e.activation(out=rstd.ap(b2, [[1, 1]], n=npart), in_=ssq.ap(b2, [[1, 1]], n=npart),
                                            func=AF.Sqrt, bias=epst.ap(0, [[1, 1]], n=npart), scale=1.0),
              reads=[('ssq', b2), 'epst'], writes=[('rstd', b2)])
        S.add('dve', lambda e: e.reciprocal(out=rstd.ap(b2, [[1, 1]], n=npart), in_=rstd.ap(b2, [[1, 1]], n=npart)),
              reads=[('rstd', b2)], writes=[('rstd', b2)])
        if m == 16:
            S.add('dve', lambda e: e.tensor_tensor(out=rstd.ap(b2, [[1, 1]], n=2), in0=rstd.ap(b2, [[1, 1]], n=2),
                                                   in1=flags.ap(0, [[1, 1]], n=2), op=ALU.mult),
                  reads=[('rstd', b2), 'flags'], writes=[('rstd', b2)])
        S.add('dve', lambda e: e.scalar_tensor_tensor(
            out=zb[b2].ap(0, [[1, D]], n=npart), in0=ht[b2].ap(0, [[1, D]], n=npart), scalar=rstd.ap(b2, [[1, 1]], n=npart),
            in1=gbc2.ap(0, [[1, D]], n=npart), op0=ALU.mult, op1=ALU.mult),
            reads=hkeys + [('rstd', b2), 'gbc2'], writes=[('zb', b2)])

    def d_back(m, pos_):
        b2 = pos_ % 2
        npart = 128 if m < 16 else 2
        pr = transpose16(zb[b2], ('zb', b2), npart, 2 + b2)
        S.add('act', lambda e: e.activation(out=zst[b2].ap(0, [[128, KC], [1, npart]]),
                                            in_=pr.ap(0, [[128, KC], [1, npart]]), func=AF.Copy),
              reads=[('psp', 2 + b2)], writes=[('zst', b2)])
        if m < 16:
            S.add('sp', lambda e: e.dma_start(out=bass.AP(zTs, 1 + 128 * m, [[KC * NQP, 128], [NQP, KC], [1, 128]]),
                                              in_=zst[b2].ap(0, [[128, KC], [1, 128]])),
                  reads=[('zst', b2)], writes=[('zTs', m)], dma=True)
        else:
            for ex in range(2):
                S.add('sp', lambda e, ex=ex: e.dma_start(out=bass.AP(zTs, ex * (NQ - 1), [[KC * NQP, 128], [NQP, KC], [1, 1]]),
                                                         in_=zst[b2].ap(ex, [[128, KC], [1, 1]]), allow_slow_non_contiguous=True),
                      reads=[('zst', b2)], writes=[('zTs', m + ex)], dma=True)

    d_order = [16] + list(range(16))
    d_xload(d_order[0], 0)
    d_xload(d_order[1], 1)
    for pos_, m in enumerate(d_order):
        d_front(m, pos_)
        if pos_ > 0:
            d_back(d_order[pos_ - 1], pos_ - 1)
    d_back(d_order[-1], 16)
    barrier()

    abytes[0] = const_mark
    bank_pool[0] = list(range(8))
    gbc3 = alloc(D * 4, F32)
    NTF = 1024
    ZW = NTF + 2
    obase = (abytes[0] + 63) // 64 * 64
    ob = [alloc(D * 4, F32) for _ in range(8)]
    zt = Reg(arena16, ARENA_BYTES // 2, obase // 2)
    assert KC * ZWP * 2 <= 8 * D * 4

    def zt_ob_overlap(k_, i_):
        a0_, a1_ = k_ * ZWP * 2, (k_ + 1) * ZWP * 2
        b0_, b1_ = i_ * D * 4, (i_ + 1) * D * 4
        return a0_ < b1_ and b0_ < a1_

    NWU = 4
    wur = alloc(NWU * KC * 128 * 2, BF)
    actT = alloc(48 * NTF * 2, BF)
    NWD = 6
    wdr = alloc(NWD * 512 * 2, BF)
    cbase_b = (abytes[0] + 63) // 64 * 64
    ctmp = [alloc(512 * 4, F32) for _ in range(4)]
    junk3 = Reg(arena16, ARENA_BYTES // 2, cbase_b // 2)
    asb = [alloc(514 * 4, F32) for _ in range(2)]
    bcast_load(gbc3, g3_d, 'gbc3')
    wuc = [0]
    wdc = [0]
    obkeys = [('ob', tt) for tt in range(8)]
    for it in range(2):
        for k0 in range(0, KC, 4):
            S.add('sp', lambda e, it=it, k0=k0: e.dma_start(
                out=zt.ap(k0 * ZWP, [[ZWP, 4], [1, ZW]]),
                in_=bass.AP(zTs, k0 * NQP + NTF * it, [[KC * NQP, 128], [NQP, 4], [1, ZW]])),
                reads=[('zTs', m) for m in range(18)],
                writes=[('zt', k_) for k_ in range(k0, k0 + 4)] +
                       [('ob', i_) for i_ in range(8) if any(zt_ob_overlap(k_, i_) for k_ in range(k0, k0 + 4))], dma=True)
        ztkeys = [('zt', k) for k in range(KC)]
        for j in range(48):
            wa = wuc[0] % NWU
            wuc[0] += 1
            wb_ = wuc[0] % NWU
            wuc[0] += 1
            wload(wur, wa * KC * 128, w_up, 2 * DFF, j * 128, 128, KC, ('wu', wa))
            wload(wur, wb_ * KC * 128, w_up, 2 * DFF, DFF + j * 128, 128, KC, ('wu', wb_))
            for half in range(2):
                q = half
                c0 = half * 512
                pra = Reg(psum[q], 1024, 0)
                pk = [('ps', 2 * q), ('ps', 2 * q + 1)]
                bb = 4 + q
                for pc in range(2):
                    for k in range(KC):
                        S.add('pe', lambda e, k=k, wa=wa, pra=pra, c0=c0, pc=pc: e.matmul(
                            pra.ap(pc * 512, [[1, 257]]), lhsT=wur.ap(wa * KC * 128 + k * 128, [[1, 128]]),
                            rhs=zt.ap(k * ZWP + c0 + pc * 257, [[1, 257]]), start=(k == 0), stop=(k == KC - 1)),
                            reads=[('wu', wa), ('zt', k)], writes=[pk[pc]])
                asq = asb[q]
                S.add('act', lambda e, pra=pra, asq=asq: e.activation(out=asq.ap(0, [[257, 2], [1, 257]]),
                                                                      in_=pra.ap(0, [[512, 2], [1, 257]]), func=AF.Copy),
                      reads=pk, writes=[('asb', q)])
                for k in range(KC):
                    S.add('pe', lambda e, k=k, wb_=wb_, bb=bb, c0=c0: e.matmul(
                        PS(bb).ap(0, [[1, 512]]), lhsT=wur.ap(wb_ * KC * 128 + k * 128, [[1, 128]]),
                        rhs=zt.ap(k * ZWP + c0 + 1, [[1, 512]]), start=(k == 0), stop=(k == KC - 1)),
                        reads=[('wu', wb_), ('zt', k)], writes=[('ps', bb)])
                c1, c2 = ctmp[q * 2], ctmp[q * 2 + 1]
                k1, k2 = ('ctmp', q * 2), ('ctmp', q * 2 + 1)
                S.add('dve', lambda e, j=j, asq=asq, c1=c1: e.tensor_scalar(
                    out=c1.ap(0, [[1, 512]]), in0=asq.ap(1, [[1, 512]]),
                    scalar1=convw.ap(48 + j, [[1, 1]]), scalar2=0.0, op0=ALU.mult, op1=ALU.add),
                    reads=[('asb', q), 'convw'], writes=[k1])
                S.add('dve', lambda e, j=j, asq=asq, c1=c1, c2=c2: e.scalar_tensor_tensor(
                    out=c2.ap(0, [[1, 512]]), in0=asq.ap(0, [[1, 512]]), scalar=convw.ap(j, [[1, 1]]), in1=c1.ap(0, [[1, 512]]),
                    op0=ALU.mult, op1=ALU.add), reads=[('asb', q), 'convw', k1], writes=[k2])
                S.add('dve', lambda e, j=j, asq=asq, c1=c1, c2=c2: e.scalar_tensor_tensor(
                    out=c1.ap(0, [[1, 512]]), in0=asq.ap(2, [[1, 512]]), scalar=convw.ap(96 + j, [[1, 1]]), in1=c2.ap(0, [[1, 512]]),
                    op0=ALU.mult, op1=ALU.add), reads=[('asb', q), 'convw', k2], writes=[k1])
                S.add('act', lambda e, j=j, c1=c1, c2=c2: e.activation(out=c2.ap(0, [[1, 512]]), in_=c1.ap(0, [[1, 512]]), func=AF.Gelu,
                                                                       bias=convb.ap(j, [[1, 1]]), scale=1.0),
                      reads=[k1, 'convb'], writes=[k2])
                S.add('dve', lambda e, j=j, c2=c2, bb=bb, c0=c0: e.tensor_tensor(
                    out=actT.ap(j * NTF + c0, [[1, 512]]), in0=PS(bb).ap(0, [[1, 512]]), in1=c2.ap(0, [[1, 512]]), op=ALU.mult),
                    reads=[('ps', bb), k2], writes=[('actT', j, half)])
        akeys = [('actT', j, hf_) for j in range(48) for hf_ in range(2)]
        for tt in range(8):
            S.add('sp', lambda e, it=it, tt=tt: e.dma_start(out=ob[tt].ap(0, [[1, D]]),
                                                            in_=bass.AP(h_s, (1 + NTF * it + 128 * tt) * D, [[D, 128], [1, D]])),
                  reads=[('h_s', m) for m in range(16)],
                  writes=[('ob', tt)] + [('zt', k_) for k_ in range(KC) if zt_ob_overlap(k_, tt)], dma=True)
        for fb in range(4):
            for c in range(48):
                wd_i = wdc[0] % NWD
                wdc[0] += 1
                S.add('pool', lambda e, c=c, fb=fb, wd_i=wd_i: e.dma_start(
                    out=wdr.ap(wd_i * 512, [[1, 512]]), in_=bass.AP(w_down, c * 128 * D + fb * 512, [[D, 128], [1, 512]])),
                    writes=[('wd', wd_i)], dma=True)
                for tt in range(8):
                    S.add('pe', lambda e, c=c, tt=tt, wd_i=wd_i: e.matmul(
                        PS(tt).ap(0, [[1, 512]]), lhsT=actT.ap(c * NTF + tt * 128, [[1, 128]]),
                        rhs=wdr.ap(wd_i * 512, [[1, 512]]), start=(c == 0), stop=(c == 47)),
                        reads=[('wd', wd_i)] + (akeys if (c == 0 and fb == 0) else []), writes=[('ps', tt)])
            for tt in range(8):
                S.add('dve', lambda e, tt=tt, fb=fb: e.tensor_tensor(
                    out=ob[tt].ap(fb * 512, [[1, 512]]), in0=PS(tt).ap(0, [[1, 512]]),
                    in1=ob[tt].ap(fb * 512, [[1, 512]]), op=ALU.add),
                    reads=[('ps', tt), ('ob', tt)], writes=[('ob', tt)])
        for tt in range(8):
            si = tt
            S.add('act', lambda e, tt=tt, si=si: e.activation(out=junk3.ap(0, [[1, D]]), in_=ob[tt].ap(0, [[1, D]]),
                                                              func=AF.Square, scale=float(D) ** -0.5, accum_out=ssq.ap(si, [[1, 1]])),
                  reads=[('ob', tt)], writes=['junk3', ('ctmp', 0), ('ctmp', 1), ('ssq', si)])
            S.add('act', lambda e, si=si: e.activation(out=rstd.ap(si, [[1, 1]]), in_=ssq.ap(si, [[1, 1]]),
                                                       func=AF.Sqrt, bias=epst.ap(0, [[1, 1]]), scale=1.0),
                  reads=[('ssq', si), 'epst'], writes=[('rstd', si)])
            S.add('dve', lambda e, si=si: e.reciprocal(out=rstd.ap(si, [[1, 1]]), in_=rstd.ap(si, [[1, 1]])),
                  reads=[('rstd', si)], writes=[('rstd', si)])
            S.add('dve', lambda e, tt=tt, si=si: e.scalar_tensor_tensor(
                out=ob[tt].ap(0, [[1, D]]), in0=ob[tt].ap(0, [[1, D]]), scalar=rstd.ap(si, [[1, 1]]),
                in1=gbc3.ap(0, [[1, D]]), op0=ALU.mult, op1=ALU.mult),
                reads=[('ob', tt), ('rstd', si), 'gbc3'], writes=[('ob', tt)])
            S.add('sp', lambda e, it=it, tt=tt: e.dma_start(out=bass.AP(out_d, (NTF * it + 128 * tt) * D, [[D, 128], [1, D]]),
                                                            in_=ob[tt].ap(0, [[1, D]])),
                  reads=[('ob', tt)], writes=[('out', it, tt)], dma=True)
    outkeys = [('out', it, tt) for it in range(2) for tt in range(8)] + (['dbg1', 'dbg2', 'dbg3'] if debug else [])
    S.add('sp', None, reads=outkeys)
    S.emit(nc, st)
    st.close()
    return nc


def host_consts(hf):
    T0 = OWN * hf
    t = np.arange(-HL, NCOL - HL)
    pos = T0 + t
    valid = (pos >= 0) & (pos < S_LEN)
    posc = np.clip(pos, 0, S_LEN - 1).astype(np.float32)
    inv_freq = (np.float32(500000.0) ** (-np.arange(0, 32, 2, dtype=np.float32) / np.float32(32))).astype(np.float32)
    ang = posc[:, None] * inv_freq[None, :]
    cos = np.cos(ang).astype(np.float32).T
    sin = np.sin(ang).astype(np.float32).T
    tab = np.zeros((2, 32, NCOL), np.float32)
    tab[0, :16] = cos
    tab[0, 16:] = cos
    tab[1, :16] = -sin
    tab[1, 16:] = sin
    lv = (hf == 1)
    rv = (hf == 0)
    kk = np.arange(128)[:, None]
    qq = np.arange(128)[None, :]
    Ag = kk >= qq
    Bg = kk <= qq
    Af = Ag & ((kk >= 64) | lv)
    Bl = Bg & ((kk < 64) | rv)
    pairs = [(Af, Bg), (Ag, Bg), (Ag, Bl), (Af, Bl)]
    maskp = np.concatenate([np.concatenate([a, b], axis=1) for a, b in pairs], axis=1).astype(np.float32)
    k1 = np.arange(128)
    mx = np.zeros((128, 4), np.float32)
    mx[:, 0] = np.where(k1 < 65, float(lv), 1.0)
    mx[:, 1] = (k1 == 0).astype(np.float32)
    mx[:, 2] = np.where(k1 < 64, 1.0, float(rv))
    mx[:, 3] = (k1 == 127).astype(np.float32) * float(rv)
    tq = np.arange(-16, NQA - 16)
    pq = T0 + tq
    vq = (pq >= 0) & (pq < S_LEN)
    invcnt = np.zeros((4, NQA), np.float32)
    for gi, w in enumerate((2, 4, 8, 16)):
        lo = np.maximum(pq - w // 2, 0)
        hi = np.minimum(pq + w // 2, S_LEN)
        cnt = np.maximum(hi - lo, 1).astype(np.float32)
        invcnt[gi] = np.where(vq, np.float32(1.0) / cnt, 0.0)
    flags = np.zeros((128, 1), np.float32)
    flags[0, 0] = float(lv)
    flags[1, 0] = float(rv)
    pm = np.zeros((128, 32), np.float32)
    for m in range(32):
        pm[(m + 16) % 32, m] = 1.0
    return dict(rot_tab=tab, maskp=maskp.astype(ml_dtypes.bfloat16), maskx=mx.astype(ml_dtypes.bfloat16),
                invcnt=invcnt, flags=flags, ident=np.eye(128, dtype=np.float32).astype(ml_dtypes.bfloat16),
                permm=pm.astype(ml_dtypes.bfloat16))


def make_in_maps(inputs):
    x = np.asarray(inputs['x'], np.float32)
    shared = {k: np.ascontiguousarray(np.asarray(inputs[k], np.float32)) for k in
              ('norm_mix_g', 'norm_ffn_g', 'norm_final_g', 'w_in', 'w_attn_out', 'pool_w', 'pool_scale', 'w_pool_out',
               'w_gate', 'w_out', 'w_up', 'conv_w', 'conv_b', 'w_down')}
    hc = [host_consts(0), host_consts(1)]
    in_maps = []
    for c in range(8):
        b, hf = c // 2, c % 2
        T0 = OWN * hf
        xe = np.zeros((NCOL, D), np.float32)
        lo, hi = T0 - HL, T0 - HL + NCOL
        slo, shi = max(lo, 0), min(hi, S_LEN)
        xe[slo - lo:shi - lo] = x[b, slo:shi]
        m = dict(shared)
        m.update(hc[hf])
        m['x_ext'] = xe
        in_maps.append(m)
    return in_maps


_NC_CACHE = {}


def kernel(**inputs):
    if 'nc' not in _NC_CACHE:
        _NC_CACHE['nc'] = build_program(False)
    nc = _NC_CACHE['nc']
    in_maps = make_in_maps(inputs)
    res = run_bass_kernel_spmd(nc, in_maps, core_ids=list(range(8)))
    out = np.zeros((4, S_LEN, D), np.float32)
    for c in range(8):
        b, hf = c // 2, c % 2
        out[b, hf * OWN:(hf + 1) * OWN] = res.results[c]['out']
    return out
```

```python
import numpy as np
import ml_dtypes
from contextlib import ExitStack
import concourse.bass as bass
import concourse.mybir as mybir
from concourse.bass_utils import run_bass_kernel_spmd

F32 = mybir.dt.float32
BF = mybir.dt.bfloat16
AF = mybir.ActivationFunctionType
ALU = mybir.AluOpType

ENGS = ('pe', 'act', 'dve', 'pool', 'sp')
DMA_RING = {'sp': 24, 'pool': 24, 'act': 4}
SEM_LIM = 20000

D = 2048
KC = 16
S_LEN = 4096
OWN = 2048
HL = 1088
NCOL = 4224
TW = 256
NT = 17
QA0 = HL - 16
NQA = 2080
NQ = 2050
DILS = (1, 4, 16)
DFF = 6144
EPS = 1e-6


class Op:
    __slots__ = ('eng', 'fn', 'deps', 'idx', 'mark', 'dma', 'dk', 'mno')


class Sched:
    def __init__(self):
        self.ops = {e: [] for e in ENGS}
        self.w = {}
        self.r = {}
        self.ndma = {e: 0 for e in ENGS}

    def add(self, eng, fn, reads=(), writes=(), dma=False):
        op = Op()
        op.eng = eng
        op.fn = fn
        op.dma = dma
        op.mark = False
        op.mno = -1
        reads = list(reads) + ['PHASE']
        deps = []
        for k in reads:
            deps += self.w.get(k, [])
        for k in writes:
            deps += self.w.get(k, [])
            rr = self.r.get(k)
            if rr:
                deps += list(rr[0].values()) + rr[1]
        seen = set()
        op.deps = []
        for d in deps:
            if id(d) not in seen and d is not op:
                seen.add(id(d))
                op.deps.append(d)
        for k in reads:
            rr = self.r.setdefault(k, [{}, []])
            if dma:
                rr[1].append(op)
            else:
                rr[0][eng] = op
        for k in writes:
            self.w[k] = [op]
            self.r[k] = [{}, []]
        if dma:
            op.dk = self.ndma[eng]
            self.ndma[eng] += 1
        op.idx = len(self.ops[eng])
        self.ops[eng].append(op)
        return op

    def emit(self, nc, stack):
        for e in ENGS:
            for op in self.ops[e]:
                for d in op.deps:
                    if not d.dma:
                        if d.eng == 'pe' and op.eng == 'pe':
                            continue
                        d.mark = True
        nmark = {}
        for e in ENGS:
            n = 0
            for op in self.ops[e]:
                if op.mark and not op.dma:
                    op.mno = n
                    n += 1
            nmark[e] = n
        esem = {}
        for e in ENGS:
            ns = (nmark[e] + SEM_LIM - 1) // SEM_LIM
            esem[e] = [stack.enter_context(nc.semaphore("tl_%s_%d" % (e, i))) for i in range(ns)]
        dsem = {}
        for e in ENGS:
            if self.ndma[e]:
                R = min(DMA_RING[e], self.ndma[e])
                dsem[e] = [stack.enter_context(nc.semaphore("dm_%s_%d" % (e, i))) for i in range(R)]
        handles = {'pe': 'tensor', 'act': 'scalar', 'dve': 'vector', 'pool': 'gpsimd', 'sp': 'sync'}
        sched = self
        block = stack.enter_context(nc.Block())

        def make(e):
            def body(h):
                waited_c = {}
                waited_d = {}
                for op in sched.ops[e]:
                    need_c = {}
                    need_d = {}
                    for d in op.deps:
                        if d.dma:
                            R = len(dsem[d.eng])
                            key = (d.eng, d.dk % R)
                            val = 16 * (d.dk // R + 1)
                            if waited_d.get(key, 0) < val:
                                need_d[key] = max(need_d.get(key, 0), val)
                        else:
                            if d.eng == 'pe' and e == 'pe':
                                continue
                            if waited_c.get(d.eng, -1) < d.mno:
                                need_c[d.eng] = max(need_c.get(d.eng, -1), d.mno)
                    if op.dma:
                        R = len(dsem[e])
                        if op.dk >= R:
                            key = (e, op.dk % R)
                            val = 16 * (op.dk // R)
                            if waited_d.get(key, 0) < val:
                                need_d[key] = max(need_d.get(key, 0), val)
                    for pe_, m in need_c.items():
                        h.wait_ge(esem[pe_][m // SEM_LIM], m % SEM_LIM + 1)
                        waited_c[pe_] = m
                    for key, val in need_d.items():
                        h.wait_ge(dsem[key[0]][key[1]], val)
                        waited_d[key] = val
                    if op.fn is None:
                        continue
                    ins = op.fn(h)
                    if op.dma:
                        R = len(dsem[e])
                        ins.then_inc(dsem[e][op.dk % R], 16)
                    elif op.mark:
                        ins.then_inc(esem[e][op.mno // SEM_LIM], 1)
            return body

        for e in ENGS:
            if sched.ops[e]:
                getattr(block, handles[e])(make(e))


class Reg:
    def __init__(self, handle, F, base):
        self.t = handle
        self.F = F
        self.base = base

    def ap(self, off=0, dims=None, p0=0, n=128):
        return bass.AP(self.t, p0 * self.F + self.base + off, [[self.F, n]] + [list(d) for d in dims])


ARENA_BYTES = 206 * 1024


def build_program(debug=False):
    nc = bass.Bass("TRN2", target_bir_lowering=False)
    dt_in = lambda name, shape, dt=F32: nc.dram_tensor(name, shape, dt, kind="ExternalInput")
    x_ext = dt_in("x_ext", [NCOL, D])
    tab_d = dt_in("rot_tab", [2, 32, NCOL])
    maskp_d = dt_in("maskp", [128, 4 * 256], BF)
    maskx_d = dt_in("maskx", [128, 4], BF)
    invcnt_d = dt_in("invcnt", [4, NQA])
    flags_d = dt_in("flags", [128, 1])
    ident_d = dt_in("ident", [128, 128], BF)
    pm_d = dt_in("permm", [128, 32], BF)
    g1_d = dt_in("norm_mix_g", [D])
    g2_d = dt_in("norm_ffn_g", [D])
    g3_d = dt_in("norm_final_g", [D])
    w_in = dt_in("w_in", [D, 5120])
    w_ao = dt_in("w_attn_out", [512, D])
    pool_w = dt_in("pool_w", [4, 128, 128])
    pool_scale = dt_in("pool_scale", [512])
    w_po = dt_in("w_pool_out", [512, D])
    w_gate = dt_in("w_gate", [D, 2 * D])
    w_out = dt_in("w_out", [D, D])
    w_up = dt_in("w_up", [D, 2 * DFF])
    conv_w = dt_in("conv_w", [3, DFF])
    conv_b = dt_in("conv_b", [DFF])
    w_down = dt_in("w_down", [DFF, D])
    out_d = nc.dram_tensor("out", [OWN, D], F32, kind="ExternalOutput")
    uTs = nc.dram_tensor("uT_s", [128, KC * NCOL], BF)
    NQP = 2080
    ZWP = 1056
    zTs = nc.dram_tensor("zT_s", [128, KC * NQP], BF)
    h_s = nc.dram_tensor("h_s", [NQ, D], F32)
    dbg = {}
    if debug:
        dbg['attnT'] = nc.dram_tensor("dbg_attnT", [128, 4 * NQ], BF, kind="ExternalOutput")
        dbg['pmT'] = nc.dram_tensor("dbg_pmT", [128, 4 * NQ], BF, kind="ExternalOutput")
        dbg['mixT'] = nc.dram_tensor("dbg_mixT", [128, KC * NQ], BF, kind="ExternalOutput")

    S = Sched()
    st = ExitStack()
    arena16 = st.enter_context(nc.sbuf_tensor("arena", [128, ARENA_BYTES // 2], BF))
    arena32 = arena16.bitcast(F32)
    psum = [st.enter_context(nc.psum_tensor("psb%d" % i, [128, 1024], F32)) for i in range(4)]
    psum16 = [p.bitcast(BF) for p in psum]

    def PS(bank, dt=F32):
        if dt == F32:
            return Reg(psum[bank // 2], 1024, (bank % 2) * 512)
        return Reg(psum16[bank // 2], 2048, (bank % 2) * 1024)

    bank_ctr = [0]
    bank_pool = [list(range(8))]

    def nbank():
        bp = bank_pool[0]
        b = bp[bank_ctr[0] % len(bp)]
        bank_ctr[0] += 1
        return b

    abytes = [0]

    def alloc(nbytes, dt):
        off = (abytes[0] + 63) // 64 * 64
        abytes[0] = off + nbytes
        assert abytes[0] <= ARENA_BYTES, ("arena overflow", abytes[0])
        if dt == F32:
            return Reg(arena32, ARENA_BYTES // 4, off // 4)
        return Reg(arena16, ARENA_BYTES // 2, off // 2)

    def barrier():
        S.add('dve', lambda e: e.memset(bar_t.ap(0, [[1, 1]]), 0.0), writes=['PHASE'])

    bar_t = alloc(4, F32)
    idt = alloc(256, BF)
    pmm = alloc(64, BF)
    ones = alloc(256, BF)
    maskp = alloc(4 * 512, BF)
    maskx = alloc(8, BF)
    epst = alloc(4, F32)
    flags = alloc(4, F32)
    pscale = alloc(16, F32)
    convw = alloc(3 * 48 * 4, F32)
    convb = alloc(48 * 4, F32)
    ssq = alloc(4 * 8, F32)
    rstd = alloc(4 * 8, F32)
    const_mark = abytes[0]
    attnT = alloc(4 * NQ * 2, BF)
    pmT = alloc(4 * NQ * 2, BF)
    base_mark = abytes[0]

    def dma_in(eng, dst_ap, src_ap, wkeys, rkeys=()):
        S.add(eng, lambda e: e.dma_start(out=dst_ap, in_=src_ap), reads=rkeys, writes=wkeys, dma=True)

    dma_in('sp', idt.ap(0, [[1, 128]]), ident_d.ap(), ['idt'])
    dma_in('sp', pmm.ap(0, [[1, 32]]), pm_d.ap(), ['pmm'])
    dma_in('sp', maskp.ap(0, [[1, 1024]]), maskp_d.ap(), ['maskp'])
    dma_in('sp', maskx.ap(0, [[1, 4]]), maskx_d.ap(), ['maskx'])
    dma_in('sp', flags.ap(0, [[1, 1]]), flags_d.ap(), ['flags'])
    def late_param_loads_pool():
        S.add('sp', lambda e: e.dma_start(out=pscale.ap(0, [[1, 4]]), in_=bass.AP(pool_scale, 0, [[1, 128], [128, 4]]),
                                          allow_slow_non_contiguous=True), writes=['pscale'], dma=True)
        for pg in range(4):
            S.add('sp', lambda e, pg=pg: e.dma_start(out=icnt4[pg].ap(0, [[1, NQA]]),
                                                     in_=bass.AP(invcnt_d, pg * NQA, [[0, 128], [1, NQA]])),
                  writes=[('icnt', pg)], dma=True)

    def late_param_loads_conv():
        S.add('sp', lambda e: e.dma_start(out=convw.ap(0, [[48, 3], [1, 48]]),
                                          in_=bass.AP(conv_w, 0, [[1, 128], [DFF, 3], [128, 48]]), allow_slow_non_contiguous=True),
              writes=['convw'], dma=True)
        S.add('sp', lambda e: e.dma_start(out=convb.ap(0, [[1, 48]]), in_=bass.AP(conv_b, 0, [[1, 128], [128, 48]]),
                                          allow_slow_non_contiguous=True), writes=['convb'], dma=True)
    S.add('dve', lambda e: e.memset(ones.ap(0, [[1, 128]]), 1.0), writes=['ones'])
    S.add('dve', lambda e: e.memset(epst.ap(0, [[1, 1]]), EPS), writes=['epst'])

    def wload(dst, doff, W, ld, col0, ncols, nk, key, row0=0):
        S.add('pool', lambda e: e.dma_start(out=dst.ap(doff, [[ncols, nk], [1, ncols]]),
                                            in_=bass.AP(W, row0 * ld + col0, [[ld, 128], [128 * ld, nk], [1, ncols]])),
              writes=[key], dma=True)

    def bcast_load(dst, vec, key):
        S.add('sp', lambda e: e.dma_start(out=dst.ap(0, [[1, D]]), in_=bass.AP(vec, 0, [[0, 128], [1, D]])),
              writes=[key], dma=True)

    def rms_rstd(src, srckey, junk, si, extra_flag=False, npart=128, skip_recip=False):
        S.add('act', lambda e: e.activation(out=junk.ap(0, [[1, D]], n=npart), in_=src.ap(0, [[1, D]], n=npart),
                                            func=AF.Square, scale=float(D) ** -0.5,
                                            accum_out=ssq.ap(si, [[1, 1]], n=npart)),
              reads=[srckey], writes=['junk', ('ssq', si)])
        S.add('act', lambda e: e.activation(out=rstd.ap(si, [[1, 1]], n=npart), in_=ssq.ap(si, [[1, 1]], n=npart),
                                            func=AF.Sqrt, bias=epst.ap(0, [[1, 1]], n=npart), scale=1.0),
              reads=[('ssq', si), 'epst'], writes=[('rstd', si)])
        if skip_recip:
            return
        S.add('dve', lambda e: e.reciprocal(out=rstd.ap(si, [[1, 1]], n=npart), in_=rstd.ap(si, [[1, 1]], n=npart)),
              reads=[('rstd', si)], writes=[('rstd', si)])
        if extra_flag:
            S.add('dve', lambda e: e.tensor_tensor(out=rstd.ap(si, [[1, 1]], n=npart), in0=rstd.ap(si, [[1, 1]], n=npart),
                                                   in1=flags.ap(0, [[1, 1]], n=npart), op=ALU.mult),
                  reads=[('rstd', si), 'flags'], writes=[('rstd', si)])

    def transpose16(src, srckey, npart, bankpair):
        pr = Reg(psum16[bankpair], 2048, 0)
        for j in range(KC):
            S.add('pe', lambda e, j=j: e.transpose(out=pr.ap(j * 128, [[1, npart]]),
                                                   in_=src.ap(j * 128, [[1, 128]], n=npart),
                                                   identity=idt.ap(0, [[1, npart]], n=npart)),
                  reads=[srckey, 'idt'], writes=[('psp', bankpair)])
        return pr

    markA = abytes[0]
    gbc = alloc(D * 4, F32)
    junk = alloc(D * 2, BF)
    XB = 3
    xt = [alloc(D * 4, F32) for _ in range(XB)]
    ub = [alloc(D * 2, BF) for _ in range(2)]
    ustage = [alloc(KC * TW * 2, BF) for _ in range(2)]
    save_ab = abytes[0]
    assert abytes[0] <= 100 * 1024
    abytes[0] = 100 * 1024
    Sa = alloc(NQA * 4, F32)
    Sb = alloc(NQA * 4, F32)
    icnt4 = [alloc(NQA * 4, F32) for _ in range(4)]
    pooled = alloc(NQ * 2, BF)
    pw_t = alloc(4 * 128 * 2, BF)
    pwts = alloc(4 * KC * 128 * 2, BF)
    p32 = alloc(4 * NQA * 4, F32)
    abytes[0] = save_ab
    bank_pool[0] = [4, 5, 6, 7]
    for pg in range(4):
        wload(pwts, pg * KC * 128, w_in, 5120, 4608 + pg * 128, 128, KC, ('pw', pg))
    S.add('pool', lambda e: e.dma_start(out=pw_t.ap(0, [[128, 4], [1, 128]]),
                                        in_=bass.AP(pool_w, 0, [[128, 128], [128 * 128, 4], [1, 128]])),
          writes=['pw_t'], dma=True)
    bcast_load(gbc, g1_d, 'gbc')
    NTA = NCOL // 128

    def pool_tail():
        for pg, wd in enumerate((2, 4, 8, 16)):
            pk = [('p32', pg, i) for i in range(NT)]
            icnt = icnt4[pg]
            src, srck = p32, pk
            soff = pg * NQA
            sh = 1
            v0 = 0
            bufs = [(Sa, 'Sa'), (Sb, 'Sb')]
            bi_ = 0
            while sh < wd:
                dst, dk = bufs[bi_]
                bi_ ^= 1
                nn = NQA - v0 - sh
                S.add('dve', lambda e, dst=dst, src=src, soff=soff, sh=sh, v0=v0, nn=nn: e.tensor_tensor(
                    out=dst.ap(v0 + sh, [[1, nn]]), in0=src.ap(soff + v0 + sh, [[1, nn]]), in1=src.ap(soff + v0, [[1, nn]]),
                    op=ALU.add), reads=list(srck), writes=[dk])
                src, srck, soff = dst, [dk], 0
                v0 += sh
                sh *= 2
            o = 15 + wd // 2 - 1
            assert o >= v0
            other, ok_ = bufs[bi_]
            S.add('dve', lambda e, src=src, o=o, other=other, icnt=icnt: e.tensor_tensor(
                out=other.ap(0, [[1, NQ]]), in0=src.ap(o, [[1, NQ]]), in1=icnt.ap(15, [[1, NQ]]), op=ALU.mult),
                reads=list(srck) + [('icnt', pg)], writes=[ok_])
            S.add('dve', lambda e, other=other, pg=pg: e.tensor_tensor(
                out=pooled.ap(0, [[1, NQ]]), in0=other.ap(0, [[1, NQ]]), in1=p32.ap(pg * NQA + 15, [[1, NQ]]), op=ALU.subtract),
                reads=[ok_] + pk, writes=['pooled'])
            for tq in range(5):
                bank = nbank()
                S.add('pe', lambda e, tq=tq, bank=bank, pg=pg: e.matmul(PS(bank).ap(0, [[1, 410]]), lhsT=pw_t.ap(pg * 128, [[1, 128]]),
                                                                        rhs=pooled.ap(tq * 410, [[1, 410]]), start=True, stop=True),
                      reads=['pw_t', 'pooled'], writes=[('ps', bank)])
                S.add('act', lambda e, tq=tq, bank=bank, pg=pg: e.activation(
                    out=pmT.ap(pg * NQ + tq * 410, [[1, 410]]), in_=PS(bank).ap(0, [[1, 410]]), func=AF.Copy,
                    scale=pscale.ap(pg, [[1, 1]])), reads=[('ps', bank), 'pscale'], writes=[('pmT', pg)])

    pendA = []

    def pool_proj(ti, sbuf_i):
        c0 = max(ti * TW, QA0)
        c1 = min(ti * TW + TW, QA0 + NQA, NCOL)
        if c1 <= c0:
            return
        a0, n = c0 - ti * TW, c1 - c0
        for pg in range(4):
            bank = nbank()
            for k in range(KC):
                S.add('pe', lambda e, k=k, bank=bank, pg=pg: e.matmul(
                    PS(bank).ap(0, [[1, n]]), lhsT=pwts.ap(pg * KC * 128 + k * 128, [[1, 128]]),
                    rhs=ustage[sbuf_i].ap(k * TW + a0, [[1, n]]), start=(k == 0), stop=(k == KC - 1)),
                    reads=[('pw', pg), ('ustage', sbuf_i, 0), ('ustage', sbuf_i, 1)], writes=[('ps', bank)])
            def pevac(pg=pg, bank=bank):
                if pg % 2 == 0:
                    S.add('act', lambda e: e.activation(
                        out=p32.ap(pg * NQA + c0 - QA0, [[1, n]]), in_=PS(bank).ap(0, [[1, n]]), func=AF.Copy),
                        reads=[('ps', bank)], writes=[('p32', pg, ti)])
                else:
                    S.add('dve', lambda e: e.tensor_copy(
                        out=p32.ap(pg * NQA + c0 - QA0, [[1, n]]), in_=PS(bank).ap(0, [[1, n]])),
                        reads=[('ps', bank)], writes=[('p32', pg, ti)])
            pendA.append(pevac)
        if c1 == QA0 + NQA:
            pendA.append(pool_tail)

    def a_load(tt):
        xb = tt % XB
        dma_in('sp', xt[xb].ap(0, [[1, D]]), bass.AP(x_ext, tt * 128 * D, [[D, 128], [1, D]]), [('xt', xb)])

    def a_front1(tt):
        rms_rstd(xt[tt % XB], ('xt', tt % XB), junk, tt % 2, skip_recip=True)

    def a_front2(tt):
        b2 = tt % 2
        xb = tt % XB
        S.add('dve', lambda e: e.reciprocal(out=rstd.ap(b2, [[1, 1]]), in_=rstd.ap(b2, [[1, 1]])),
              reads=[('rstd', b2)], writes=[('rstd', b2)])
        S.add('dve', lambda e: e.scalar_tensor_tensor(out=ub[b2].ap(0, [[1, D]]), in0=xt[xb].ap(0, [[1, D]]),
                                                      scalar=rstd.ap(b2, [[1, 1]]), in1=gbc.ap(0, [[1, D]]),
                                                      op0=ALU.mult, op1=ALU.mult),
              reads=[('xt', xb), ('rstd', b2), 'gbc'], writes=[('ub', b2)])
        return transpose16(ub[b2], ('ub', b2), 128, b2)

    def a_back_pp(tt):
        if tt % 2 == 1 or tt == NTA - 1:
            pool_proj(tt // 2, (tt // 2) % 2)

    def a_back_evac(tt, pr):
        b2 = tt % 2
        ti = tt // 2
        half = tt % 2
        sbuf_i = ti % 2
        if tt % 2 == 0:
            S.add('act', lambda e: e.activation(
                out=ustage[sbuf_i].ap(half * 128, [[TW, KC], [1, 128]]), in_=pr.ap(0, [[128, KC], [1, 128]]), func=AF.Copy),
                reads=[('psp', b2)], writes=[('ustage', sbuf_i, half)])
        else:
            S.add('dve', lambda e: e.tensor_copy(
                out=ustage[sbuf_i].ap(half * 128, [[TW, KC], [1, 128]]), in_=pr.ap(0, [[128, KC], [1, 128]])),
                reads=[('psp', b2)], writes=[('ustage', sbuf_i, half)])
        if half == 1 or tt == NTA - 1:
            wcols = 128 * (half + 1)
            S.add('sp', lambda e: e.dma_start(
                out=bass.AP(uTs, ti * TW, [[KC * NCOL, 128], [NCOL, KC], [1, wcols]]),
                in_=ustage[sbuf_i].ap(0, [[TW, KC], [1, wcols]])),
                reads=[('ustage', sbuf_i, 0), ('ustage', sbuf_i, 1)], writes=[('uTs', ti)], dma=True)

    def flushA():
        while pendA:
            pendA.pop(0)()

    a_load(0)
    a_load(1)
    prevA = None
    for tt in range(NTA):
        if tt + 2 < NTA:
            a_load(tt + 2)
        if tt == 3:
            late_param_loads_pool()
        a_front1(tt)
        if prevA is not None:
            a_back_evac(*prevA)
        prA = a_front2(tt)
        flushA()
        if prevA is not None:
            a_back_pp(prevA[0])
        prevA = (tt, prA)
    a_back_evac(*prevA)
    flushA()
    a_back_pp(prevA[0])
    flushA()
    barrier()

    abytes[0] = markA
    bank_pool[0] = list(range(8))
    NWR = 12
    wring = alloc(NWR * KC * 128 * 2, BF)
    wctr = [0]

    def wslot():
        i = wctr[0] % NWR
        wctr[0] += 1
        return i

    utile = [alloc(KC * TW * 2, BF) for _ in range(3)]
    assert abytes[0] >= markA + KC * NQ * 2
    uTm_pre = Reg(arena16, ARENA_BYTES // 2, ((markA + 63) // 64 * 64) // 2)
    tabt = [alloc(2 * TW * 4, F32) for _ in range(3)]
    qraw = [alloc(TW * 2, BF) for _ in range(4)]
    rtmpA = alloc(TW * 4, F32)
    rtmpB = alloc(TW * 4, F32)
    markB = abytes[0]
    utc = [0]

    def load_utile(i):
        w = min(TW, NCOL - i * TW)
        bi = utc[0] % 3
        utc[0] += 1
        S.add('sp', lambda e: e.dma_start(out=utile[bi].ap(0, [[TW, KC], [1, w]]),
                                          in_=bass.AP(uTs, i * TW, [[KC * NCOL, 128], [NCOL, KC], [1, w]])),
              reads=[('uTs', i)], writes=[('utile', bi)], dma=True)
        S.add('sp', lambda e: e.dma_start(out=tabt[bi].ap(0, [[TW, 2], [1, w]], n=32),
                                          in_=bass.AP(tab_d, i * TW, [[NCOL, 32], [32 * NCOL, 2], [1, w]])),
              writes=[('tabt', bi)], dma=True)
        return bi, w

    def proj_chunk(bi, a0, n, wi):
        bank = nbank()
        pr = PS(bank)
        for k in range(KC):
            S.add('pe', lambda e, k=k: e.matmul(pr.ap(0, [[1, n]]), lhsT=wring.ap(wi * KC * 128 + k * 128, [[1, 128]]),
                                                rhs=utile[bi].ap(k * TW + a0, [[1, n]]), start=(k == 0), stop=(k == KC - 1)),
                  reads=[('w', wi), ('utile', bi)], writes=[('ps', bank)])
        return bank

    abytes[0] = markB
    LQ = {d: NQA // d for d in DILS}
    LK = {d: OWN // d + 130 for d in DILS}
    K0 = {d: HL - 65 * d for d in DILS}
    NK = {d: OWN + 130 * d for d in DILS}
    qT = {d: alloc(NQA * 2, BF) for d in DILS}
    kT = {d: alloc(NK[d] * 2, BF) for d in DILS}
    vT = {d: alloc(NK[d] * 2, BF) for d in DILS}
    NVC = 20
    Vc = alloc(NVC * 128 * 2, BF)
    acc = alloc(2 * NQA * 4, F32)
    pT = [alloc(256 * 2, BF) for _ in range(5)]
    rden = alloc(NQ * 4, F32)
    SCALE = 128.0 ** -0.5

    rtA = [rtmpA, alloc(TW * 4, F32), alloc(TW * 4, F32), alloc(TW * 4, F32)]
    rtB = [rtmpB, alloc(TW * 4, F32), alloc(TW * 4, F32), alloc(TW * 4, F32)]
    rctr = [0]
    pend = []

    def flush(keep=0):
        while len(pend) > keep:
            pend.pop(0)()

    def rot_evac(bank, n, bi, a0, dst, d, j0, kind_key):
        nj = n // d
        L = LK[d] if dst is kT[d] else LQ[d]
        qi = rctr[0] % 4
        rctr[0] += 1
        S.add('dve', lambda e: e.tensor_copy(out=qraw[qi].ap(0, [[1, n]]), in_=PS(bank).ap(0, [[1, n]])),
              reads=[('ps', bank)], writes=[('qraw', qi)])

        def part2():
            b2 = nbank()
            S.add('pe', lambda e: e.matmul(PS(b2).ap(0, [[1, n]], n=32), lhsT=pmm.ap(0, [[1, 32]]), rhs=qraw[qi].ap(0, [[1, n]]),
                                           start=True, stop=True), reads=['pmm', ('qraw', qi)], writes=[('ps', b2)])
            S.add('act', lambda e: e.activation(out=dst.ap(j0, [[L, d], [1, nj]]),
                                                in_=qraw[qi].ap(0, [[1, d], [d, nj]]), func=AF.Copy),
                  reads=[('qraw', qi)], writes=[kind_key])
            S.add('dve', lambda e: e.tensor_tensor(out=rtA[qi].ap(0, [[1, n]], n=32), in0=qraw[qi].ap(0, [[1, n]], n=32),
                                                   in1=tabt[bi].ap(a0, [[1, n]], n=32), op=ALU.mult),
                  reads=[('qraw', qi), ('tabt', bi)], writes=[('rtA', qi)])
            S.add('dve', lambda e: e.tensor_tensor(out=rtB[qi].ap(0, [[1, n]], n=32), in0=PS(b2).ap(0, [[1, n]], n=32),
                                                   in1=tabt[bi].ap(TW + a0, [[1, n]], n=32), op=ALU.mult),
                  reads=[('ps', b2), ('tabt', bi)], writes=[('rtB', qi)])
            S.add('dve', lambda e: e.tensor_tensor(out=dst.ap(j0, [[L, d], [1, nj]], n=32), in0=rtA[qi].ap(0, [[1, d], [d, nj]], n=32),
                                                   in1=rtB[qi].ap(0, [[1, d], [d, nj]], n=32), op=ALU.add),
                  reads=[('rtA', qi), ('rtB', qi)], writes=[kind_key])
        pend.append(part2)

    kT_keys = {d: [] for d in DILS}
    vT_keys = {d: [] for d in DILS}
    qT_keys = {d: [] for d in DILS}
    def load_slot_weights(s_):
        wm = {}
        order = [(2, 'k'), (2, 'v'), (1, 'k'), (1, 'v'), (0, 'k'), (0, 'v'), (2, 'q'), (1, 'q'), (0, 'q')]
        cb = {'k': 1536, 'v': 3072, 'q': 0}
        for g, kind in order:
            hh = g * 4 + s_
            wi = wslot()
            wload(wring, wi * KC * 128, w_in, 5120, cb[kind] + hh * 128, 128, KC, ('w', wi))
            wm[(g, kind)] = wi
        return wm

    wmap_next = load_slot_weights(0)
    for s in range(4):
        wmap = wmap_next
        for d in DILS:
            kT_keys[d], vT_keys[d], qT_keys[d] = [], [], []
        for i in range(NT):
            t0, t1 = i * TW, min(i * TW + TW, NCOL)
            todo = []
            for g, d in enumerate(DILS):
                c0, c1 = max(t0, K0[d]), min(t1, K0[d] + NK[d])
                if c1 > c0:
                    todo.append((g, d, 'k', c0, c1))
                    todo.append((g, d, 'v', c0, c1))
                c0, c1 = max(t0, QA0), min(t1, QA0 + NQA)
                if c1 > c0:
                    todo.append((g, d, 'q', c0, c1))
            if not todo:
                continue
            bi, w = load_utile(i)
            for (g, d, kind, c0, c1) in todo:
                a0, n = c0 - t0, c1 - c0
                bank = proj_chunk(bi, a0, n, wmap[(g, kind)])
                flush(keep=2)
                if kind == 'v':
                    j0 = (c0 - K0[d]) // d
                    nj = n // d
                    vT_keys[d].append(('vT', d, i))
                    S.add('act', lambda e, bank=bank, d=d, j0=j0, nj=nj, n=n: e.activation(
                        out=vT[d].ap(j0, [[LK[d], d], [1, nj]]), in_=PS(bank).ap(0, [[1, d], [d, nj]]), func=AF.Copy),
                        reads=[('ps', bank)], writes=[('vT', d, i)])
                elif kind == 'k':
                    kT_keys[d].append(('kT', d, i))
                    rot_evac(bank, n, bi, a0, kT[d], d, (c0 - K0[d]) // d, ('kT', d, i))
                else:
                    qT_keys[d].append(('qT', d, i))
                    rot_evac(bank, n, bi, a0, qT[d], d, (c0 - QA0) // d, ('qT', d, i))
        flush()
        if s + 1 < 4:
            wmap_next = load_slot_weights(s + 1)
        else:
            alias = [('w', i_) for i_ in range(NWR)] + [('utile', b_) for b_ in range(3)] + [('tabt', b_) for b_ in range(3)]
            for k in range(KC):
                S.add('sp', lambda e, k=k: e.dma_start(out=uTm_pre.ap(k * NQ, [[1, NQ]]),
                                                       in_=bass.AP(uTs, k * NCOL + HL - 1, [[KC * NCOL, 128], [1, NQ]])),
                      reads=[('uTs', i_) for i_ in range(NT)], writes=[('uTm', k)] + (alias if k == 0 else []), dma=True)

        vctr = [0]

        def vchunk(d, r, jk0):
            slot = vctr[0] % NVC
            vctr[0] += 1
            bank = nbank()
            pr = PS(bank, BF)
            S.add('pe', lambda e: e.transpose(out=pr.ap(0, [[1, 128]]), in_=vT[d].ap(r * LK[d] + jk0, [[1, 128]]),
                                              identity=idt.ap(0, [[1, 128]])),
                  reads=list(vT_keys[d]) + ['idt'], writes=[('ps', bank)])
            S.add('act', lambda e: e.activation(out=Vc.ap(slot * 128, [[1, 128]]), in_=pr.ap(0, [[1, 128]]), func=AF.Copy),
                  reads=[('ps', bank)], writes=[('Vc', slot)])
            return slot

        blk = [0]

        def attn_block(d, r, jq0, N, chunks, mask_ap, maskkey, acc_off, acc_step, first, akeys):
            bs = nbank()
            for h, (jk0, vs) in enumerate(chunks):
                S.add('pe', lambda e, h=h, jk0=jk0: e.matmul(PS(bs).ap(h * N, [[1, N]]), lhsT=kT[d].ap(r * LK[d] + jk0, [[1, 128]]),
                                                             rhs=qT[d].ap(r * LQ[d] + jq0, [[1, N]]), start=True, stop=True),
                      reads=list(kT_keys[d]) + list(qT_keys[d]), writes=[('ps', bs)])
            pi = blk[0] % 5
            blk[0] += 1
            flush(keep=3)
            S.add('act', lambda e: e.activation(out=pT[pi].ap(0, [[1, 2 * N]]), in_=PS(bs).ap(0, [[1, 2 * N]]), func=AF.Exp, scale=SCALE),
                  reads=[('ps', bs)], writes=[('pT', pi)])
            S.add('dve', lambda e: e.tensor_tensor(out=pT[pi].ap(0, [[1, 2 * N]]), in0=pT[pi].ap(0, [[1, 2 * N]]), in1=mask_ap, op=ALU.mult),
                  reads=[('pT', pi), maskkey], writes=[('pT', pi)])

            def stage2():
                bn = nbank()
                for h, (jk0, vs) in enumerate(chunks):
                    S.add('pe', lambda e, h=h, vs=vs: e.matmul(PS(bn).ap(0, [[1, N]]), lhsT=Vc.ap(vs * 128, [[1, 128]]),
                                                               rhs=pT[pi].ap(h * N, [[1, N]]), start=(h == 0), stop=(h == 1)),
                          reads=[('Vc', vs), ('pT', pi)], writes=[('ps', bn)])
                for h in range(2):
                    S.add('pe', lambda e, h=h: e.matmul(PS(bn).ap(N, [[1, N]]), lhsT=ones.ap(0, [[1, 128]]),
                                                        rhs=pT[pi].ap(h * N, [[1, N]]), start=(h == 0), stop=(h == 1)),
                          reads=['ones', ('pT', pi)], writes=[('ps', bn)])
                if first:
                    S.add('dve', lambda e: e.tensor_copy(out=acc.ap(acc_off, [[NQA, 2], [acc_step, N]]), in_=PS(bn).ap(0, [[N, 2], [1, N]])),
                          reads=[('ps', bn)], writes=akeys)
                else:
                    S.add('dve', lambda e: e.tensor_tensor(out=acc.ap(acc_off, [[NQA, 2], [acc_step, N]]), in0=PS(bn).ap(0, [[N, 2], [1, N]]),
                                                           in1=acc.ap(acc_off, [[NQA, 2], [acc_step, N]]), op=ALU.add),
                          reads=[('ps', bn)] + akeys, writes=akeys)
            pend.append(stage2)

        for g, d in enumerate(DILS):
            L = OWN // d
            nb = L // 128
            lq0 = 16 // d
            for r in range(d):
                vs = [vchunk(d, r, 128 * c + 1) for c in range(nb + 1)]
                for b in range(nb):
                    mp = (3 if nb == 1 else 0) if b == 0 else (2 if b == nb - 1 else 1)
                    akeys = [('acc', d * b + jj) for jj in range(d)]
                    attn_block(d, r, lq0 + 128 * b, 128, [(128 * b + 1, vs[b]), (128 * (b + 1) + 1, vs[b + 1])],
                               maskp.ap(mp * 256, [[1, 256]]), 'maskp', 16 + r + d * 128 * b, d, g == 0, akeys)
                if r == 0:
                    vx = vchunk(d, 0, L + 2)
                    attn_block(d, 0, lq0 + L, 1, [(L + 1, vs[nb]), (L + 2, vx)],
                               maskx.ap(2, [[1, 2]]), 'maskx', 16 + OWN, 1, g == 0, [('acc', 'R')])
                if r == d - 1:
                    vx1 = vchunk(d, r, 0)
                    vx2 = vchunk(d, r, 128)
                    attn_block(d, r, lq0 - 1, 1, [(0, vx1), (128, vx2)],
                               maskx.ap(0, [[1, 2]]), 'maskx', 15, 1, g == 0, [('acc', 'L')])
        flush()
        allacc = [('acc', jj) for jj in range(16)] + [('acc', 'L'), ('acc', 'R')]
        S.add('act', lambda e: e.activation(out=rden.ap(0, [[1, NQ]]), in_=acc.ap(NQA + 15, [[1, NQ]]), func=AF.Ln),
              reads=allacc, writes=['rden'])
        S.add('act', lambda e: e.activation(out=rden.ap(0, [[1, NQ]]), in_=rden.ap(0, [[1, NQ]]), func=AF.Exp, scale=-1.0),
              reads=['rden'], writes=['rden'])
        S.add('dve', lambda e, s=s: e.tensor_tensor(out=attnT.ap(s * NQ, [[1, NQ]]), in0=acc.ap(15, [[1, NQ]]),
                                                    in1=rden.ap(0, [[1, NQ]]), op=ALU.mult),
              reads=allacc + ['rden'], writes=[('attnT', s)])
    barrier()
    if debug:
        S.add('sp', lambda e: e.dma_start(out=dbg['attnT'].ap(), in_=attnT.ap(0, [[1, 4 * NQ]])),
              reads=[('attnT', s) for s in range(4)], writes=['dbg1'], dma=True)
        S.add('sp', lambda e: e.dma_start(out=dbg['pmT'].ap(), in_=pmT.ap(0, [[1, 4 * NQ]])),
              reads=[('pmT', s) for s in range(4)], writes=['dbg2'], dma=True)

    abytes[0] = base_mark
    MIXOFF = (ARENA_BYTES - KC * NQ * 2) // 64 * 64
    mixT = Reg(arena16, ARENA_BYTES // 2, MIXOFF // 2)
    uTm = alloc(KC * NQ * 2, BF)
    NWC = 3
    wgr = alloc(NWC * (2 * KC * 128 + 2 * 4 * 128) * 2, BF)
    WGS = 2 * KC * 128 + 2 * 4 * 128
    gst = [alloc(410 * 4, F32) for _ in range(4)]
    assert abytes[0] <= MIXOFF, abytes[0]
    CQ0 = HL - 1
    assert uTm.base == uTm_pre.base and base_mark == markA
    late_param_loads_conv()
    for f in range(KC):
        wi = f % NWC
        wb = wi * WGS
        wload(wgr, wb, w_gate, 2 * D, f * 128, 128, KC, ('wg', wi, 0))
        wload(wgr, wb + KC * 128, w_gate, 2 * D, D + f * 128, 128, KC, ('wg', wi, 1))
        wload(wgr, wb + 2 * KC * 128, w_ao, D, f * 128, 128, 4, ('wg', wi, 2))
        wload(wgr, wb + 2 * KC * 128 + 512, w_po, D, f * 128, 128, 4, ('wg', wi, 3))
        for tq in range(5):
            q0 = tq * 410
            banks = []
            for part in range(2):
                bank = nbank()
                for k in range(KC):
                    S.add('pe', lambda e, k=k, bank=bank, part=part, wb=wb, q0=q0: e.matmul(
                        PS(bank).ap(0, [[1, 410]]), lhsT=wgr.ap(wb + part * KC * 128 + k * 128, [[1, 128]]),
                        rhs=uTm.ap(k * NQ + q0, [[1, 410]]), start=(k == 0), stop=(k == KC - 1)),
                        reads=[('wg', wi, part), ('uTm', k)], writes=[('ps', bank)])
                S.add('act', lambda e, bank=bank, part=part: e.activation(out=gst[part].ap(0, [[1, 410]]), in_=PS(bank).ap(0, [[1, 410]]),
                                                                          func=AF.Sigmoid),
                      reads=[('ps', bank)], writes=[('gst', part)])
                banks.append(bank)
            for part, (srcT, skey) in enumerate(((attnT, 'attnT'), (pmT, 'pmT'))):
                bank = nbank()
                for k in range(4):
                    S.add('pe', lambda e, k=k, bank=bank, part=part, srcT=srcT, wb=wb, q0=q0: e.matmul(
                        PS(bank).ap(0, [[1, 410]]), lhsT=wgr.ap(wb + 2 * KC * 128 + part * 512 + k * 128, [[1, 128]]),
                        rhs=srcT.ap(k * NQ + q0, [[1, 410]]), start=(k == 0), stop=(k == 3)),
                        reads=[('wg', wi, 2 + part)] + [(skey, s_) for s_ in range(4)], writes=[('ps', bank)])
                S.add('dve', lambda e, bank=bank, part=part: e.tensor_tensor(
                    out=gst[2 + part].ap(0, [[1, 410]]), in0=PS(bank).ap(0, [[1, 410]]), in1=gst[part].ap(0, [[1, 410]]), op=ALU.mult),
                    reads=[('ps', bank), ('gst', part)], writes=[('gst', 2 + part)])
            S.add('dve', lambda e, f=f, q0=q0: e.tensor_tensor(out=mixT.ap(f * NQ + q0, [[1, 410]]), in0=gst[2].ap(0, [[1, 410]]),
                                                               in1=gst[3].ap(0, [[1, 410]]), op=ALU.add),
                  reads=[('gst', 2), ('gst', 3)], writes=[('mixT', f)])
    barrier()
    if debug:
        S.add('sp', lambda e: e.dma_start(out=dbg['mixT'].ap(), in_=mixT.ap(0, [[1, KC * NQ]])),
              reads=[('mixT', f) for f in range(KC)], writes=['dbg3'], dma=True)

    abytes[0] = const_mark
    bank_pool[0] = [0, 1, 2, 3]
    wo = alloc(KC * D * 2, BF)
    gbc2 = alloc(D * 4, F32)
    junk2 = alloc(D * 2, BF)
    xt2 = [alloc(D * 4, F32) for _ in range(3)]
    ht = [alloc(D * 4, F32) for _ in range(2)]
    zb = [alloc(D * 2, BF) for _ in range(2)]
    zst = [alloc(KC * 128 * 2, BF) for _ in range(2)]
    assert abytes[0] <= MIXOFF, abytes[0]
    for fb in range(4):
        S.add('pool', lambda e, fb=fb: e.dma_start(out=wo.ap(fb * 512, [[D, KC], [1, 512]]),
                                                   in_=bass.AP(w_out, fb * 512, [[D, 128], [128 * D, KC], [1, 512]])),
              writes=[('wo', fb)], dma=True)
    bcast_load(gbc2, g2_d, 'gbc2')
    mixkeys = [('mixT', f) for f in range(KC)]

    def d_xload(m, pos_):
        xb = pos_ % 3
        if m < 16:
            dma_in('sp', xt2[xb].ap(0, [[1, D]]), bass.AP(x_ext, (HL + 128 * m) * D, [[D, 128], [1, D]]), [('xt2', xb)])
        else:
            dma_in('sp', xt2[xb].ap(0, [[1, D]], n=2), bass.AP(x_ext, (HL - 1) * D, [[(OWN + 1) * D, 2], [1, D]]), [('xt2', xb)])

    def d_front(m, pos_):
        b2 = pos_ % 2
        xb = pos_ % 3
        if pos_ + 2 < 17:
            d_xload(d_order[pos_ + 2], pos_ + 2)
        if m < 16:
            npart = 128
            lcols = lambda k: mixT.ap(k * NQ + 1 + 128 * m, [[1, 128]])
        else:
            npart = 2
            lcols = lambda k: mixT.ap(k * NQ, [[NQ - 1, 2]])
        for fb in range(4):
            bank = nbank()
            for k in range(KC):
                S.add('pe', lambda e, k=k, bank=bank, fb=fb: e.matmul(
                    PS(bank).ap(0, [[1, 512]], n=npart), lhsT=lcols(k), rhs=wo.ap(k * D + fb * 512, [[1, 512]]),
                    start=(k == 0), stop=(k == KC - 1)), reads=mixkeys + [('wo', fb)], writes=[('ps', bank)])
            S.add('dve', lambda e, bank=bank, fb=fb: e.tensor_tensor(
                out=ht[b2].ap(fb * 512, [[1, 512]], n=npart), in0=PS(bank).ap(0, [[1, 512]], n=npart),
                in1=xt2[xb].ap(fb * 512, [[1, 512]], n=npart), op=ALU.add),
                reads=[('ps', bank), ('xt2', xb)], writes=[('ht', b2, fb)])
        hkeys = [('ht', b2, fb) for fb in range(4)]
        if m < 16:
            S.add('sp', lambda e: e.dma_start(out=bass.AP(h_s, (1 + 128 * m) * D, [[D, 128], [1, D]]),
                                              in_=ht[b2].ap(0, [[1, D]])),
                  reads=hkeys, writes=[('h_s', m)], dma=True)
        S.add('act', lambda e: e.activation(out=junk2.ap(0, [[1, D]], n=npart), in_=ht[b2].ap(0, [[1, D]], n=npart),
                                            func=AF.Square, scale=float(D) ** -0.5,
                                            accum_out=ssq.ap(b2, [[1, 1]], n=npart)),
              reads=hkeys, writes=['junk2', ('ssq', b2)])
        S.add('act', lambda e: e.activation(out=rstd.ap(b2, [[1, 1]], n=npart), in_=ssq.ap(b2, [[1, 1]], n=npart),
                                            func=AF.Sqrt, bias=epst.ap(0, [[1, 1]], n=npart), scale=1.0),
              reads=[('ssq', b2), 'epst'], writes=[('rstd', b2)])
        S.add('dve', lambda e: e.reciprocal(out=rstd.ap(b2, [[1, 1]], n=npart), in_=rstd.ap(b2, [[1, 1]], n=npart)),
              reads=[('rstd', b2)], writes=[('rstd', b2)])
        if m == 16:
            S.add('dve', lambda e: e.tensor_tensor(out=rstd.ap(b2, [[1, 1]], n=2), in0=rstd.ap(b2, [[1, 1]], n=2),
                                                   in1=flags.ap(0, [[1, 1]], n=2), op=ALU.mult),
                  reads=[('rstd', b2), 'flags'], writes=[('rstd', b2)])
        S.add('dve', lambda e: e.scalar_tensor_tensor(
            out=zb[b2].ap(0, [[1, D]], n=npart), in0=ht[b2].ap(0, [[1, D]], n=npart), scalar=rstd.ap(b2, [[1, 1]], n=npart),
            in1=gbc2.ap(0, [[1, D]], n=npart), op0=ALU.mult, op1=ALU.mult),
            reads=hkeys + [('rstd', b2), 'gbc2'], writes=[('zb', b2)])

    def d_back(m, pos_):
        b2 = pos_ % 2
        npart = 128 if m < 16 else 2
        pr = transpose16(zb[b2], ('zb', b2), npart, 2 + b2)
        S.add('act', lambda e: e.activation(out=zst[b2].ap(0, [[128, KC], [1, npart]]),
                                            in_=pr.ap(0, [[128, KC], [1, npart]]), func=AF.Copy),
              reads=[('psp', 2 + b2)], writes=[('zst', b2)])
        if m < 16:
            S.add('sp', lambda e: e.dma_start(out=bass.AP(zTs, 1 + 128 * m, [[KC * NQP, 128], [NQP, KC], [1, 128]]),
                                              in_=zst[b2].ap(0, [[128, KC], [1, 128]])),
                  reads=[('zst', b2)], writes=[('zTs', m)], dma=True)
        else:
            for ex in range(2):
                S.add('sp', lambda e, ex=ex: e.dma_start(out=bass.AP(zTs, ex * (NQ - 1), [[KC * NQP, 128], [NQP, KC], [1, 1]]),
                                                         in_=zst[b2].ap(ex, [[128, KC], [1, 1]]), allow_slow_non_contiguous=True),
                      reads=[('zst', b2)], writes=[('zTs', m + ex)], dma=True)

    d_order = [16] + list(range(16))
    d_xload(d_order[0], 0)
    d_xload(d_order[1], 1)
    for pos_, m in enumerate(d_order):
        d_front(m, pos_)
        if pos_ > 0:
            d_back(d_order[pos_ - 1], pos_ - 1)
    d_back(d_order[-1], 16)
    barrier()

    abytes[0] = const_mark
    bank_pool[0] = list(range(8))
    gbc3 = alloc(D * 4, F32)
    NTF = 1024
    ZW = NTF + 2
    obase = (abytes[0] + 63) // 64 * 64
    ob = [alloc(D * 4, F32) for _ in range(8)]
    zt = Reg(arena16, ARENA_BYTES // 2, obase // 2)
    assert KC * ZWP * 2 <= 8 * D * 4

    def zt_ob_overlap(k_, i_):
        a0_, a1_ = k_ * ZWP * 2, (k_ + 1) * ZWP * 2
        b0_, b1_ = i_ * D * 4, (i_ + 1) * D * 4
        return a0_ < b1_ and b0_ < a1_

    NWU = 4
    wur = alloc(NWU * KC * 128 * 2, BF)
    actT = alloc(48 * NTF * 2, BF)
    NWD = 6
    wdr = alloc(NWD * 512 * 2, BF)
    cbase_b = (abytes[0] + 63) // 64 * 64
    ctmp = [alloc(512 * 4, F32) for _ in range(4)]
    junk3 = Reg(arena16, ARENA_BYTES // 2, cbase_b // 2)
    asb = [alloc(514 * 4, F32) for _ in range(2)]
    bcast_load(gbc3, g3_d, 'gbc3')
    wuc = [0]
    wdc = [0]
    obkeys = [('ob', tt) for tt in range(8)]
    for it in range(2):
        for k0 in range(0, KC, 4):
            S.add('sp', lambda e, it=it, k0=k0: e.dma_start(
                out=zt.ap(k0 * ZWP, [[ZWP, 4], [1, ZW]]),
                in_=bass.AP(zTs, k0 * NQP + NTF * it, [[KC * NQP, 128], [NQP, 4], [1, ZW]])),
                reads=[('zTs', m) for m in range(18)],
                writes=[('zt', k_) for k_ in range(k0, k0 + 4)] +
                       [('ob', i_) for i_ in range(8) if any(zt_ob_overlap(k_, i_) for k_ in range(k0, k0 + 4))], dma=True)
        ztkeys = [('zt', k) for k in range(KC)]
        for j in range(48):
            wa = wuc[0] % NWU
            wuc[0] += 1
            wb_ = wuc[0] % NWU
            wuc[0] += 1
            wload(wur, wa * KC * 128, w_up, 2 * DFF, j * 128, 128, KC, ('wu', wa))
            wload(wur, wb_ * KC * 128, w_up, 2 * DFF, DFF + j * 128, 128, KC, ('wu', wb_))
            for half in range(2):
                q = half
                c0 = half * 512
                pra = Reg(psum[q], 1024, 0)
                pk = [('ps', 2 * q), ('ps', 2 * q + 1)]
                bb = 4 + q
                for pc in range(2):
                    for k in range(KC):
                        S.add('pe', lambda e, k=k, wa=wa, pra=pra, c0=c0, pc=pc: e.matmul(
                            pra.ap(pc * 512, [[1, 257]]), lhsT=wur.ap(wa * KC * 128 + k * 128, [[1, 128]]),
                            rhs=zt.ap(k * ZWP + c0 + pc * 257, [[1, 257]]), start=(k == 0), stop=(k == KC - 1)),
                            reads=[('wu', wa), ('zt', k)], writes=[pk[pc]])
                asq = asb[q]
                S.add('act', lambda e, pra=pra, asq=asq: e.activation(out=asq.ap(0, [[257, 2], [1, 257]]),
                                                                      in_=pra.ap(0, [[512, 2], [1, 257]]), func=AF.Copy),
                      reads=pk, writes=[('asb', q)])
                for k in range(KC):
                    S.add('pe', lambda e, k=k, wb_=wb_, bb=bb, c0=c0: e.matmul(
                        PS(bb).ap(0, [[1, 512]]), lhsT=wur.ap(wb_ * KC * 128 + k * 128, [[1, 128]]),
                        rhs=zt.ap(k * ZWP + c0 + 1, [[1, 512]]), start=(k == 0), stop=(k == KC - 1)),
                        reads=[('wu', wb_), ('zt', k)], writes=[('ps', bb)])
                c1, c2 = ctmp[q * 2], ctmp[q * 2 + 1]
                k1, k2 = ('ctmp', q * 2), ('ctmp', q * 2 + 1)
                S.add('dve', lambda e, j=j, asq=asq, c1=c1: e.tensor_scalar(
                    out=c1.ap(0, [[1, 512]]), in0=asq.ap(1, [[1, 512]]),
                    scalar1=convw.ap(48 + j, [[1, 1]]), scalar2=0.0, op0=ALU.mult, op1=ALU.add),
                    reads=[('asb', q), 'convw'], writes=[k1])
                S.add('dve', lambda e, j=j, asq=asq, c1=c1, c2=c2: e.scalar_tensor_tensor(
                    out=c2.ap(0, [[1, 512]]), in0=asq.ap(0, [[1, 512]]), scalar=convw.ap(j, [[1, 1]]), in1=c1.ap(0, [[1, 512]]),
                    op0=ALU.mult, op1=ALU.add), reads=[('asb', q), 'convw', k1], writes=[k2])
                S.add('dve', lambda e, j=j, asq=asq, c1=c1, c2=c2: e.scalar_tensor_tensor(
                    out=c1.ap(0, [[1, 512]]), in0=asq.ap(2, [[1, 512]]), scalar=convw.ap(96 + j, [[1, 1]]), in1=c2.ap(0, [[1, 512]]),
                    op0=ALU.mult, op1=ALU.add), reads=[('asb', q), 'convw', k2], writes=[k1])
                S.add('act', lambda e, j=j, c1=c1, c2=c2: e.activation(out=c2.ap(0, [[1, 512]]), in_=c1.ap(0, [[1, 512]]), func=AF.Gelu,
                                                                       bias=convb.ap(j, [[1, 1]]), scale=1.0),
                      reads=[k1, 'convb'], writes=[k2])
                S.add('dve', lambda e, j=j, c2=c2, bb=bb, c0=c0: e.tensor_tensor(
                    out=actT.ap(j * NTF + c0, [[1, 512]]), in0=PS(bb).ap(0, [[1, 512]]), in1=c2.ap(0, [[1, 512]]), op=ALU.mult),
                    reads=[('ps', bb), k2], writes=[('actT', j, half)])
        akeys = [('actT', j, hf_) for j in range(48) for hf_ in range(2)]
        for tt in range(8):
            S.add('sp', lambda e, it=it, tt=tt: e.dma_start(out=ob[tt].ap(0, [[1, D]]),
                                                            in_=bass.AP(h_s, (1 + NTF * it + 128 * tt) * D, [[D, 128], [1, D]])),
                  reads=[('h_s', m) for m in range(16)],
                  writes=[('ob', tt)] + [('zt', k_) for k_ in range(KC) if zt_ob_overlap(k_, tt)], dma=True)
        for fb in range(4):
            for c in range(48):
                wd_i = wdc[0] % NWD
                wdc[0] += 1
                S.add('pool', lambda e, c=c, fb=fb, wd_i=wd_i: e.dma_start(
                    out=wdr.ap(wd_i * 512, [[1, 512]]), in_=bass.AP(w_down, c * 128 * D + fb * 512, [[D, 128], [1, 512]])),
                    writes=[('wd', wd_i)], dma=True)
                for tt in range(8):
                    S.add('pe', lambda e, c=c, tt=tt, wd_i=wd_i: e.matmul(
                        PS(tt).ap(0, [[1, 512]]), lhsT=actT.ap(c * NTF + tt * 128, [[1, 128]]),
                        rhs=wdr.ap(wd_i * 512, [[1, 512]]), start=(c == 0), stop=(c == 47)),
                        reads=[('wd', wd_i)] + (akeys if (c == 0 and fb == 0) else []), writes=[('ps', tt)])
            for tt in range(8):
                S.add('dve', lambda e, tt=tt, fb=fb: e.tensor_tensor(
                    out=ob[tt].ap(fb * 512, [[1, 512]]), in0=PS(tt).ap(0, [[1, 512]]),
                    in1=ob[tt].ap(fb * 512, [[1, 512]]), op=ALU.add),
                    reads=[('ps', tt), ('ob', tt)], writes=[('ob', tt)])
        for tt in range(8):
            si = tt
            S.add('act', lambda e, tt=tt, si=si: e.activation(out=junk3.ap(0, [[1, D]]), in_=ob[tt].ap(0, [[1, D]]),
                                                              func=AF.Square, scale=float(D) ** -0.5, accum_out=ssq.ap(si, [[1, 1]])),
                  reads=[('ob', tt)], writes=['junk3', ('ctmp', 0), ('ctmp', 1), ('ssq', si)])
            S.add('act', lambda e, si=si: e.activation(out=rstd.ap(si, [[1, 1]]), in_=ssq.ap(si, [[1, 1]]),
                                                       func=AF.Sqrt, bias=epst.ap(0, [[1, 1]]), scale=1.0),
                  reads=[('ssq', si), 'epst'], writes=[('rstd', si)])
            S.add('dve', lambda e, si=si: e.reciprocal(out=rstd.ap(si, [[1, 1]]), in_=rstd.ap(si, [[1, 1]])),
                  reads=[('rstd', si)], writes=[('rstd', si)])
            S.add('dve', lambda e, tt=tt, si=si: e.scalar_tensor_tensor(
                out=ob[tt].ap(0, [[1, D]]), in0=ob[tt].ap(0, [[1, D]]), scalar=rstd.ap(si, [[1, 1]]),
                in1=gbc3.ap(0, [[1, D]]), op0=ALU.mult, op1=ALU.mult),
                reads=[('ob', tt), ('rstd', si), 'gbc3'], writes=[('ob', tt)])
            S.add('sp', lambda e, it=it, tt=tt: e.dma_start(out=bass.AP(out_d, (NTF * it + 128 * tt) * D, [[D, 128], [1, D]]),
                                                            in_=ob[tt].ap(0, [[1, D]])),
                  reads=[('ob', tt)], writes=[('out', it, tt)], dma=True)
    outkeys = [('out', it, tt) for it in range(2) for tt in range(8)] + (['dbg1', 'dbg2', 'dbg3'] if debug else [])
    S.add('sp', None, reads=outkeys)
    S.emit(nc, st)
    st.close()
    return nc


def host_consts(hf):
    T0 = OWN * hf
    t = np.arange(-HL, NCOL - HL)
    pos = T0 + t
    valid = (pos >= 0) & (pos < S_LEN)
    posc = np.clip(pos, 0, S_LEN - 1).astype(np.float32)
    inv_freq = (np.float32(500000.0) ** (-np.arange(0, 32, 2, dtype=np.float32) / np.float32(32))).astype(np.float32)
    ang = posc[:, None] * inv_freq[None, :]
    cos = np.cos(ang).astype(np.float32).T
    sin = np.sin(ang).astype(np.float32).T
    tab = np.zeros((2, 32, NCOL), np.float32)
    tab[0, :16] = cos
    tab[0, 16:] = cos
    tab[1, :16] = -sin
    tab[1, 16:] = sin
    lv = (hf == 1)
    rv = (hf == 0)
    kk = np.arange(128)[:, None]
    qq = np.arange(128)[None, :]
    Ag = kk >= qq
    Bg = kk <= qq
    Af = Ag & ((kk >= 64) | lv)
    Bl = Bg & ((kk < 64) | rv)
    pairs = [(Af, Bg), (Ag, Bg), (Ag, Bl), (Af, Bl)]
    maskp = np.concatenate([np.concatenate([a, b], axis=1) for a, b in pairs], axis=1).astype(np.float32)
    k1 = np.arange(128)
    mx = np.zeros((128, 4), np.float32)
    mx[:, 0] = np.where(k1 < 65, float(lv), 1.0)
    mx[:, 1] = (k1 == 0).astype(np.float32)
    mx[:, 2] = np.where(k1 < 64, 1.0, float(rv))
    mx[:, 3] = (k1 == 127).astype(np.float32) * float(rv)
    tq = np.arange(-16, NQA - 16)
    pq = T0 + tq
    vq = (pq >= 0) & (pq < S_LEN)
    invcnt = np.zeros((4, NQA), np.float32)
    for gi, w in enumerate((2, 4, 8, 16)):
        lo = np.maximum(pq - w // 2, 0)
        hi = np.minimum(pq + w // 2, S_LEN)
        cnt = np.maximum(hi - lo, 1).astype(np.float32)
        invcnt[gi] = np.where(vq, np.float32(1.0) / cnt, 0.0)
    flags = np.zeros((128, 1), np.float32)
    flags[0, 0] = float(lv)
    flags[1, 0] = float(rv)
    pm = np.zeros((128, 32), np.float32)
    for m in range(32):
        pm[(m + 16) % 32, m] = 1.0
    return dict(rot_tab=tab, maskp=maskp.astype(ml_dtypes.bfloat16), maskx=mx.astype(ml_dtypes.bfloat16),
                invcnt=invcnt, flags=flags, ident=np.eye(128, dtype=np.float32).astype(ml_dtypes.bfloat16),
                permm=pm.astype(ml_dtypes.bfloat16))


def make_in_maps(inputs):
    x = np.asarray(inputs['x'], np.float32)
    shared = {k: np.ascontiguousarray(np.asarray(inputs[k], np.float32)) for k in
              ('norm_mix_g', 'norm_ffn_g', 'norm_final_g', 'w_in', 'w_attn_out', 'pool_w', 'pool_scale', 'w_pool_out',
               'w_gate', 'w_out', 'w_up', 'conv_w', 'conv_b', 'w_down')}
    hc = [host_consts(0), host_consts(1)]
    in_maps = []
    for c in range(8):
        b, hf = c // 2, c % 2
        T0 = OWN * hf
        xe = np.zeros((NCOL, D), np.float32)
        lo, hi = T0 - HL, T0 - HL + NCOL
        slo, shi = max(lo, 0), min(hi, S_LEN)
        xe[slo - lo:shi - lo] = x[b, slo:shi]
        m = dict(shared)
        m.update(hc[hf])
        m['x_ext'] = xe
        in_maps.append(m)
    return in_maps


_NC_CACHE = {}


def kernel(**inputs):
    if 'nc' not in _NC_CACHE:
        _NC_CACHE['nc'] = build_program(False)
    nc = _NC_CACHE['nc']
    in_maps = make_in_maps(inputs)
    res = run_bass_kernel_spmd(nc, in_maps, core_ids=list(range(8)))
    out = np.zeros((4, S_LEN, D), np.float32)
    for c in range(8):
        b, hf = c // 2, c % 2
        out[b, hf * OWN:(hf + 1) * OWN] = res.results[c]['out']
    return out
```
